# Optimizing a Trainium2 kernel written in Bass

```python
import math
import jax, jax.numpy as jnp
from jax import lax
import numpy as np

D_MODEL = 1024
BATCH = 4
SEQ = 8192
DEPTH = 1

N_MEM = 256
EPS = 1e-6
NEG = -1e30

A_HEADS = 16
A_HEAD_DIM = 64
A_WIDTH = A_HEADS * A_HEAD_DIM
MOBA_BLOCK = 256
MOBA_TOPK = 3
MOBA_Q_CHUNK = 32
ROPE_THETA = 10000.0

B_HEADS = 8
B_KEY_DIM = 128
B_VAL_DIM = D_MODEL // B_HEADS
B_KEY_WIDTH = B_HEADS * B_KEY_DIM
B_VAL_WIDTH = B_HEADS * B_VAL_DIM
HGRN_CHUNK = 64

X_HEADS = 4
X_HEAD_DIM = D_MODEL // X_HEADS

FFN_HIDDEN = ((8 * D_MODEL + 3 * 256 - 1) // (3 * 256)) * 256

IN_WIDTHS = (A_WIDTH, A_WIDTH, A_WIDTH, B_KEY_WIDTH, B_KEY_WIDTH, B_VAL_WIDTH, B_VAL_WIDTH, D_MODEL, D_MODEL)
IN_WIDTH = A_WIDTH * 3 + B_KEY_WIDTH * 2 + B_VAL_WIDTH * 2 + D_MODEL * 2

kernel_name = "hybrid_moba_hgrn2_griffin_merge"


def rmsnorm(x, g):
    xf = x.astype(jnp.float32)
    y = xf * lax.rsqrt(jnp.mean(xf * xf, axis=-1, keepdims=True) + EPS)
    return (y * g.astype(jnp.float32)).astype(x.dtype)


def rope(t, pos):
    hd = t.shape[-1]
    inv_freq = jnp.power(ROPE_THETA, -jnp.arange(0, hd, 2, dtype=jnp.float32) / hd)
    ang = pos.astype(jnp.float32)[:, None, :, None] * inv_freq
    cos, sin = jnp.cos(ang), jnp.sin(ang)
    tf = t.astype(jnp.float32)
    t1, t2 = tf[..., : hd // 2], tf[..., hd // 2:]
    return jnp.concatenate([t1 * cos - t2 * sin, t2 * cos + t1 * sin], axis=-1).astype(t.dtype)


def split_cols(z):
    outs, off = [], 0
    for w in IN_WIDTHS:
        outs.append(z[..., off:off + w])
        off += w
    return outs


def moba_attention(q, k, v, pos):
    B, S, _ = q.shape
    H, hd, BLK, QC = A_HEADS, A_HEAD_DIM, MOBA_BLOCK, MOBA_Q_CHUNK
    scale = hd ** -0.5

    def heads(t):
        return t.reshape(B, S, H, hd).transpose(0, 2, 1, 3)

    q = rope(heads(q), pos)
    k = rope(heads(k), pos)
    v = heads(v)
    nb = -(-S // BLK)
    s_pad = nb * BLK
    padw = [(0, 0), (0, 0), (0, s_pad - S), (0, 0)]
    q, k, v = jnp.pad(q, padw), jnp.pad(k, padw), jnp.pad(v, padw)
    kb = k.reshape(B, H, nb, BLK, hd)
    vb = v.reshape(B, H, nb, BLK, hd)

    k_mean = jnp.mean(kb.astype(jnp.float32), axis=3)
    gate = jnp.einsum('bhsd,bhnd->bhsn', q.astype(jnp.float32), k_mean)
    q_blk = jnp.arange(s_pad) // BLK
    past = jnp.arange(nb)[None, :] < q_blk[:, None]
    gate = jnp.where(past, gate, NEG)
    n_sel = min(MOBA_TOPK, nb)
    _, idx = lax.top_k(gate, n_sel)
    valid = idx < q_blk[:, None]

    n_chunks = s_pad // QC

    def to_chunks(t):
        return jnp.moveaxis(t.reshape(B, H, n_chunks, QC, t.shape[-1]), 2, 0)

    starts = jnp.arange(n_chunks, dtype=jnp.int32) * QC
    bi = jnp.arange(B)[:, None, None, None]
    hi = jnp.arange(H)[None, :, None, None]

    def attend(args):
        qc, ic, vmask, start = args
        blk = start // BLK
        k_own = lax.dynamic_index_in_dim(kb, blk, axis=2, keepdims=False)
        v_own = lax.dynamic_index_in_dim(vb, blk, axis=2, keepdims=False)
        q_pos = start + jnp.arange(QC)
        k_pos = blk * BLK + jnp.arange(BLK)
        s_own = jnp.einsum('bhqd,bhkd->bhqk', qc, k_own).astype(jnp.float32) * scale
        s_own = jnp.where(k_pos[None, :] <= q_pos[:, None], s_own, NEG)
        k_sel = kb[bi, hi, ic]
        v_sel = vb[bi, hi, ic]
        s_sel = jnp.einsum('bhqd,bhqjkd->bhqjk', qc, k_sel).astype(jnp.float32) * scale
        s_sel = jnp.where(vmask[..., None], s_sel, NEG).reshape(B, H, QC, n_sel * BLK)
        p = jax.nn.softmax(jnp.concatenate([s_sel, s_own], axis=-1), axis=-1).astype(v.dtype)
        p_sel = p[..., : n_sel * BLK].reshape(B, H, QC, n_sel, BLK)
        p_own = p[..., n_sel * BLK:]
        return (jnp.einsum('bhqjk,bhqjkd->bhqd', p_sel, v_sel)
                + jnp.einsum('bhqk,bhkd->bhqd', p_own, v_own))

    o = lax.map(attend, (to_chunks(q), to_chunks(idx), to_chunks(valid), starts))
    o = jnp.moveaxis(o, 0, 2).reshape(B, H, s_pad, hd)[:, :, :S]
    return o.transpose(0, 2, 1, 3).reshape(B, S, H * hd)


def hgrn2(q, f, i, og, lb, norm_g):
    B, S, _ = q.shape
    H, dk, dv, C = B_HEADS, B_KEY_DIM, B_VAL_DIM, HGRN_CHUNK
    nc = S // C

    def heads(t, d):
        return t.reshape(B, S, H, d).transpose(0, 2, 1, 3).astype(jnp.float32)

    qh = jax.nn.silu(heads(q, dk)) * (dk ** -0.5)
    lbh = lb.astype(jnp.float32).reshape(1, H, 1, dk)
    fg = lbh + (1.0 - lbh) * jax.nn.sigmoid(heads(f, dk))
    logf = jnp.log(fg)
    kh = 1.0 - fg
    vh = heads(i, dv)

    def to_chunks(t):
        return jnp.moveaxis(t.reshape(B, H, nc, C, t.shape[-1]), 2, 0)

    causal = jnp.tril(jnp.ones((C, C), dtype=bool))[:, :, None]

    def step(state, xs):
        qc, kc, vc, gc = xs
        b = jnp.cumsum(gc, axis=2)
        o_inter = jnp.einsum('bhtd,bhdv->bhtv', qc * jnp.exp(b), state)
        diff = b[:, :, :, None, :] - b[:, :, None, :, :]
        decay = jnp.where(causal, jnp.exp(jnp.where(causal, diff, 0.0)), 0.0)
        a = jnp.einsum('bhtd,bhsd,bhtsd->bhts', qc, kc, decay)
        o = o_inter + jnp.einsum('bhts,bhsv->bhtv', a, vc)
        b_last = b[:, :, -1:, :]
        new_state = (jnp.exp(b_last[:, :, 0, :])[..., None] * state
                     + jnp.einsum('bhsd,bhsv->bhdv', kc * jnp.exp(b_last - b), vc))
        return new_state, o

    s0 = jnp.zeros((B, H, dk, dv), jnp.float32)
    _, o = lax.scan(step, s0, (to_chunks(qh), to_chunks(kh), to_chunks(vh), to_chunks(logf)))
    o = jnp.moveaxis(o, 0, 2).reshape(B, H, S, dv).transpose(0, 2, 1, 3)
    o = o * lax.rsqrt(jnp.mean(o * o, axis=-1, keepdims=True) + EPS)
    o = o.reshape(B, S, H * dv) * norm_g.astype(jnp.float32)
    o = o * jax.nn.silu(og.astype(jnp.float32))
    return o.astype(q.dtype)


def memory_cross_attention(h, mem, wq, wkv, wo):
    B, S, _ = h.shape
    M = mem.shape[1]
    q = (h @ wq).reshape(B, S, X_HEADS, X_HEAD_DIM)
    kv = mem @ wkv
    k = kv[..., :D_MODEL].reshape(B, M, X_HEADS, X_HEAD_DIM)
    v = kv[..., D_MODEL:].reshape(B, M, X_HEADS, X_HEAD_DIM)
    s = jnp.einsum('bshd,bmhd->bhsm', q, k).astype(jnp.float32) * (X_HEAD_DIM ** -0.5)
    p = jax.nn.softmax(s, axis=-1).astype(v.dtype)
    o = jnp.einsum('bhsm,bmhd->bshd', p, v).reshape(B, S, D_MODEL)
    return o @ wo


def setup_inputs(seed: int = 0) -> dict:
    key = jax.random.key(seed)
    ks = jax.random.split(key, 24)
    f32 = jnp.float32

    def w(k, shape, fan_in):
        return jax.random.normal(k, shape, f32) * (fan_in ** -0.5)

    def gain(k, shape):
        return 1.0 + 0.02 * jax.random.normal(k, shape, f32)

    x = jax.random.normal(ks[0], (BATCH, SEQ, D_MODEL), f32)
    mem = jax.random.normal(ks[1], (BATCH, N_MEM, D_MODEL), f32)
    offs = jax.random.randint(ks[2], (BATCH, 1), 0, 1024, dtype=jnp.int32)
    positions = offs + jnp.arange(SEQ, dtype=jnp.int32)[None, :]
    return {
        "x": x,
        "mem": mem,
        "positions": positions,
        "norm_mix_g": gain(ks[3], (DEPTH, D_MODEL)),
        "w_in": w(ks[4], (DEPTH, D_MODEL, IN_WIDTH), D_MODEL),
        "hgrn_lb_logits": 0.5 * jax.random.normal(ks[5], (DEPTH + 1, B_KEY_WIDTH), f32),
        "hgrn_norm_g": gain(ks[6], (DEPTH, B_VAL_WIDTH)),
        "w_br_attn": w(ks[7], (DEPTH, A_WIDTH, D_MODEL), A_WIDTH),
        "w_br_hgrn": w(ks[8], (DEPTH, B_VAL_WIDTH, D_MODEL), B_VAL_WIDTH),
        "w_out": w(ks[9], (DEPTH, D_MODEL, D_MODEL), D_MODEL),
        "norm_x_g": gain(ks[10], (DEPTH, D_MODEL)),
        "norm_mem_g": gain(ks[11], (DEPTH, D_MODEL)),
        "wq_x": w(ks[12], (DEPTH, D_MODEL, D_MODEL), D_MODEL),
        "wkv_x": w(ks[13], (DEPTH, D_MODEL, 2 * D_MODEL), D_MODEL),
        "wo_x": w(ks[14], (DEPTH, D_MODEL, D_MODEL), D_MODEL),
        "norm_ffn_g": gain(ks[15], (DEPTH, D_MODEL)),
        "w_ffn_in": w(ks[16], (DEPTH, D_MODEL, 2 * FFN_HIDDEN), D_MODEL),
        "w_ffn_out": w(ks[17], (DEPTH, FFN_HIDDEN, D_MODEL), FFN_HIDDEN),
        "final_norm_g": gain(ks[18], (D_MODEL,)),
    }


def reference(x, mem, positions, norm_mix_g, w_in, hgrn_lb_logits, hgrn_norm_g, w_br_attn,
              w_br_hgrn, w_out, norm_x_g, norm_mem_g, wq_x, wkv_x, wo_x, norm_ffn_g,
              w_ffn_in, w_ffn_out, final_norm_g):
    lb_all = jnp.cumsum(jax.nn.softmax(hgrn_lb_logits.astype(jnp.float32), axis=0), axis=0)
    h = x
    for l in range(DEPTH):
        xn = rmsnorm(h, norm_mix_g[l])
        q_a, k_a, v_a, q_b, f_b, i_b, og_b, g_a, g_b = split_cols(xn @ w_in[l])
        y_a = moba_attention(q_a, k_a, v_a, positions) @ w_br_attn[l]
        y_b = hgrn2(q_b, f_b, i_b, og_b, lb_all[l], hgrn_norm_g[l]) @ w_br_hgrn[l]
        merged = jax.nn.sigmoid(g_a) * y_a + jax.nn.sigmoid(g_b) * y_b
        h = h + merged @ w_out[l]
        h = h + memory_cross_attention(rmsnorm(h, norm_x_g[l]), rmsnorm(mem, norm_mem_g[l]),
                                       wq_x[l], wkv_x[l], wo_x[l])
        hn = rmsnorm(h, norm_ffn_g[l])
        gu = hn @ w_ffn_in[l]
        h = h + (jax.nn.silu(gu[..., :FFN_HIDDEN]) * gu[..., FFN_HIDDEN:]) @ w_ffn_out[l]
    return rmsnorm(h, final_norm_g)
```

```python
import os
import numpy as np
from contextlib import ExitStack
import concourse.bass as bass
import concourse.mybir as mybir
from concourse.bass_utils import run_bass_kernel_spmd

F32 = mybir.dt.float32
BF16 = mybir.dt.bfloat16
I32 = mybir.dt.int32
ALU = mybir.AluOpType
AF = mybir.ActivationFunctionType
AX = mybir.AxisListType

D = 1024
KC = 8
TP = 4096
TO = 4096
TT = TP + TO
G = 512
NG_ALL = TT // G
NG_PRE = TP // G
FH = 2816
FC = FH // 128
NMEM = 256
EPS = 1e-6
NEGM = 30000.0
PI_LO = 3.1415925
C1 = 6.28125
C2 = float(2 * np.pi - 6.28125)


class T:
    __slots__ = ("writer", "readers")

    def __init__(self):
        self.writer = None
        self.readers = []


def TL(n):
    return [T() for _ in range(n)]


class WT:
    def __init__(self, kcn, ncols):
        self.kcn = kcn
        self.b = {cb: TL(kcn) for cb in range(0, ncols, 1024)}

    def blk(self, cb):
        return self.b[(cb // 1024) * 1024]

    def rd(self, col0, kc):
        return self.b[(col0 // 1024) * 1024][kc]


class Sched:
    def __init__(self, nc, stack, n_dma_sems=24):
        self.nc = nc
        self.eng = {"pe": nc.tensor, "act": nc.scalar, "dve": nc.vector, "pool": nc.gpsimd, "sp": nc.sync}
        self.ops = {k: [] for k in self.eng}
        self.sems = {}
        for k in ("pe", "act", "dve", "pool"):
            self.sems[k] = stack.enter_context(nc.semaphore("s_" + k))
        self.count = {k: 0 for k in ("pe", "act", "dve", "pool")}
        self.waited = {k: {} for k in self.eng}
        self.dma_pool = {}
        for q in ("sp", "pool"):
            lst = []
            for i in range(n_dma_sems):
                key = "d_%s_%d" % (q, i)
                self.sems[key] = stack.enter_context(nc.semaphore(key))
                lst.append([key, 0])
            self.dma_pool[q] = lst
        self.dma_rr = {"sp": 0, "pool": 0}
        self.n_inst = 0

    def _waits(self, e, deps, skip_self=False):
        w = self.waited[e]
        best = {}
        for d in deps:
            if d is None:
                continue
            k, v = d
            if skip_self and k == e:
                continue
            if best.get(k, 0) < v:
                best[k] = v
        out = []
        for k, v in best.items():
            if w.get(k, 0) >= v:
                continue
            w[k] = v
            out.append((k, v))
        return out

    def _emit_waits(self, e, waits):
        eng = self.eng[e]
        for (k, v) in waits:
            sem = self.sems[k]
            self.ops[e].append(lambda eng=eng, sem=sem, v=v: eng.wait_ge(sem, v))
            self.n_inst += 1

    @staticmethod
    def _deps(reads, writes):
        deps = []
        for t in reads:
            deps.append(t.writer)
        for t in writes:
            deps.append(t.writer)
            deps.extend(t.readers)
        return deps

    @staticmethod
    def _update(me, reads, writes):
        for t in writes:
            t.writer = me
            t.readers = []
        for t in reads:
            t.readers.append(me)
            if len(t.readers) > 48:
                best = {}
                for (k, v) in t.readers:
                    if best.get(k, 0) < v:
                        best[k] = v
                t.readers = list(best.items())

    def op(self, e, fn, reads=(), writes=()):
        waits = self._waits(e, self._deps(reads, writes), skip_self=(e == "pe"))
        self._emit_waits(e, waits)
        self.count[e] += 1
        sem = self.sems[e]
        self.ops[e].append(lambda fn=fn, sem=sem: fn().then_inc(sem, 1))
        self.n_inst += 1
        self._update((e, self.count[e]), reads, writes)

    def dma(self, q, out_ap, in_ap, reads=(), writes=()):
        deps = self._deps(reads, writes)
        pool = self.dma_pool[q]
        slot = pool[self.dma_rr[q] % len(pool)]
        self.dma_rr[q] += 1
        key, cnt = slot
        if cnt > 0:
            deps.append((key, cnt))
        self._emit_waits(q, self._waits(q, deps))
        slot[1] = cnt + 16
        sem = self.sems[key]
        eng = self.eng[q]
        self.ops[q].append(lambda eng=eng, o=out_ap, i=in_ap, sem=sem: eng.dma_start(out=o, in_=i).then_inc(sem, 16))
        self.n_inst += 1
        self._update((key, cnt + 16), reads, writes)

    def barrier(self):
        deps = [(k, self.count[k]) for k in self.count if self.count[k] > 0]
        for q in self.dma_pool:
            for key, cnt in self.dma_pool[q]:
                if cnt > 0:
                    deps.append((key, cnt))
        for e in self.eng:
            self._emit_waits(e, self._waits(e, deps, skip_self=(e == "pe")))

    def finish(self, final_tiles):
        deps = []
        for t in final_tiles:
            deps.append(t.writer)
            deps.extend(t.readers)
        self._emit_waits("sp", self._waits("sp", deps))
        nc = self.nc
        ops = self.ops
        with nc.Block() as block:
            @block.tensor
            def _(e):
                for f in ops["pe"]:
                    f()

            @block.scalar
            def _(e):
                for f in ops["act"]:
                    f()

            @block.vector
            def _(e):
                for f in ops["dve"]:
                    f()

            @block.gpsimd
            def _(e):
                for f in ops["pool"]:
                    f()

            @block.sync
            def _(e):
                for f in ops["sp"]:
                    f()


class Arena:
    LO = 20480
    HI = 229376

    def __init__(self, nc):
        self.nc = nc
        self.off = self.LO
        self.n = 0
        self.marks = []

    def alloc(self, shape, dt):
        size = int(np.prod(shape[1:])) * mybir.dt.size(dt)
        size = (size + 63) // 64 * 64
        assert self.off + size <= self.HI, "SBUF overflow: need %d at %d" % (size, self.off)
        self.n += 1
        t = self.nc.alloc_sbuf_tensor_at("sb%d" % self.n, list(shape), dt, offset=self.off)
        self.off += size
        return t

    def alloc_at(self, off, shape, dt):
        self.n += 1
        return self.nc.alloc_sbuf_tensor_at("sb%d" % self.n, list(shape), dt, offset=off)

    def push(self):
        self.marks.append(self.off)

    def pop(self):
        self.off = self.marks.pop()


def build_program(debug=False, stop_after=99):
    nc = bass.Bass("TRN2", target_bir_lowering=False)

    def din(name, shape, dt=F32):
        return nc.dram_tensor(name, list(shape), dt, kind="ExternalInput").ap()

    skind = "ExternalOutput" if debug else "Internal"

    def dscr(name, shape, dt):
        return nc.dram_tensor(name, list(shape), dt, kind=skind).ap()

    xT = din("xT", [D, TT])
    pos = din("pos", [1, TT], I32)
    pastneg = din("pastneg", [1, 16 * 32])
    invf = din("invf", [128, 1])
    sgn = din("sgn", [128, 1])
    rotm = din("rotm", [128, 128])
    memT = din("memT", [D, NMEM])
    w_in = din("w_in", [D, 9216])
    g_mix = din("g_mix", [128, KC])
    lbl = din("lbl", [128, 2 * 8])
    g_hn = din("g_hn", [128, 8])
    w_bra = din("w_bra", [D, D])
    w_brh = din("w_brh", [D, D])
    w_out = din("w_out", [D, D])
    g_x = din("g_x", [128, KC])
    g_mem = din("g_mem", [128, KC])
    wq_x = din("wq_x", [D, D])
    wkv_x = din("wkv_x", [D, 2 * D])
    wo_x = din("wo_x", [D, D])
    g_ffn = din("g_ffn", [128, KC])
    w_fi = din("w_fi", [D, 2 * FH])
    w_fo = din("w_fo", [FH, D])
    g_fin = din("g_fin", [128, KC])
    outT = nc.dram_tensor("outT", [D, TO], F32, kind="ExternalOutput").ap()

    Ksc = dscr("Ksc", [D, TT], BF16)
    Vsc = dscr("Vsc", [TT, D], BF16)
    Qsc = dscr("Qsc", [D, TO], BF16)
    Msc = dscr("Msc", [16, 32, TO], BF16)
    Gsc = dscr("Gsc", [2 * D, TO], BF16)
    Hsc = dscr("Hsc", [D, TO], BF16)
    Asc = dscr("Asc", [D, TO], BF16)
    H2sc = dscr("H2sc", [D, TO], F32)

    t_Ksc, t_Vsc, t_Qsc, t_Msc, t_Gsc, t_Hsc, t_Asc, t_H2 = TL(8)
    t_out = T()

    with ExitStack() as st:
        S = Sched(nc, st)
        A = Arena(nc)
        PS = [st.enter_context(nc.psum_tensor("psb%d" % i, [128, 512], F32)) for i in range(8)]
        PST = TL(8)

        rr = {"n": 0}

        def alt(engs=("dve", "pool")):
            rr["n"] += 1
            return engs[rr["n"] % len(engs)]

        def E(e):
            return S.eng[e]

        ones_bf = A.alloc([128, 128], BF16); t_ones = T()
        ident_bf = A.alloc([128, 128], BF16); t_ident = T()
        ident_f = A.alloc([128, 128], F32); t_identf = T()
        tri01 = A.alloc([128, 128], BF16); t_tri = T()
        tri_f = A.alloc([128, 128], F32)
        tri64 = A.alloc([128, 64], F32); t_tri64 = T()
        scanm = A.alloc([128, G], F32); t_scanm = T()
        invf_sb = A.alloc([128, 1], F32); t_invf = T()
        sgn_sb = A.alloc([128, 1], F32); t_sgn = T()
        pastneg_sb = A.alloc([128, 512], F32); t_pastneg = T()
        gains = {}
        for nm, ap_, w in (("mix", g_mix, KC), ("x", g_x, KC), ("mem", g_mem, KC), ("ffn", g_ffn, KC), ("fin", g_fin, KC), ("hn", g_hn, 8), ("lbl", lbl, 16)):
            tt = A.alloc([128, w], F32)
            tk = T()
            S.dma("sp", tt[:], ap_, writes=[tk])
            gains[nm] = (tt, tk)
        rotm_sb = A.alloc([128, 128], F32); t_rotm = T()
        S.dma("sp", rotm_sb[:], rotm, writes=[t_rotm])
        S.dma("sp", invf_sb[:], invf, writes=[t_invf])
        S.dma("sp", sgn_sb[:], sgn, writes=[t_sgn])
        S.dma("sp", pastneg_sb[:], pastneg.partition_broadcast(128), writes=[t_pastneg])

        S.op("pool", lambda: nc.gpsimd.memset(ident_f[:], 1.0), writes=[t_identf])
        S.op("pool", lambda: nc.gpsimd.affine_select(out=ident_f[:], in_=ident_f[:], pattern=[[-1, 128]], compare_op=ALU.is_equal, fill=0.0, base=0, channel_multiplier=1), reads=[t_identf], writes=[t_identf])
        S.op("dve", lambda: nc.vector.tensor_copy(out=ident_bf[:], in_=ident_f[:]), reads=[t_identf], writes=[t_ident])
        S.op("pool", lambda: nc.gpsimd.memset(ones_bf[:], 1.0), writes=[t_ones])
        t_trif = T()
        S.op("pool", lambda: nc.gpsimd.memset(tri_f[:], 1.0), writes=[t_trif])
        S.op("pool", lambda: nc.gpsimd.affine_select(out=tri_f[:], in_=tri_f[:], pattern=[[1, 128]], compare_op=ALU.is_ge, fill=0.0, base=0, channel_multiplier=-1), reads=[t_trif], writes=[t_trif])
        S.op("dve", lambda: nc.vector.tensor_copy(out=tri01[:], in_=tri_f[:]), reads=[t_trif], writes=[t_tri])
        S.op("dve", lambda: nc.vector.tensor_copy(out=tri64[0:64, :], in_=tri_f[0:64, 0:64]), reads=[t_trif], writes=[t_tri64])
        S.op("dve", lambda: nc.vector.tensor_copy(out=tri64[64:128, :], in_=tri_f[64:128, 64:128]), reads=[t_trif], writes=[t_tri64])
        S.op("pool", lambda: nc.gpsimd.memset(scanm[:], 1.0), writes=[t_scanm])
        S.op("pool", lambda: nc.gpsimd.memset(scanm[:].rearrange("p (c t) -> p c t", t=64)[:, :, 0:1], 0.0), writes=[t_scanm])

        lbt, t_lbl = gains["lbl"]
        lb_sb = A.alloc([128, 8], F32); t_lb = T()
        oml_sb = A.alloc([128, 8], F32); t_oml = T()
        S.op("dve", lambda: nc.vector.tensor_tensor(out=lb_sb[:], in0=lbt[:, 0:8], in1=lbt[:, 8:16], op=ALU.subtract), reads=[t_lbl], writes=[t_lb])
        S.op("act", lambda: nc.scalar.activation(out=lb_sb[:], in_=lb_sb[:], func=AF.Sigmoid), reads=[t_lb], writes=[t_lb])
        S.op("dve", lambda: nc.vector.tensor_scalar(out=oml_sb[:], in0=lb_sb[:], scalar1=-1.0, scalar2=1.0, op0=ALU.mult, op1=ALU.add), reads=[t_lb], writes=[t_oml])

        STG_OFF = A.off
        STG = [A.alloc([128, 1024], F32) for _ in range(4)]
        T_STG = TL(4)
        STGI = {"i": 0}
        noml_sb = A.alloc([128, 8], F32); t_noml = T()
        S.op("dve", lambda: nc.vector.tensor_scalar(out=noml_sb[:], in0=oml_sb[:], scalar1=-1.0, scalar2=None, op0=ALU.mult), reads=[t_oml], writes=[t_noml])
        gc1 = A.alloc([128, 8], F32); gc0 = A.alloc([128, 8], F32); gnc1 = A.alloc([128, 8], F32); gomc0 = A.alloc([128, 8], F32)
        t_gc = T()
        S.op("dve", lambda: nc.vector.tensor_scalar(out=gc1[:], in0=oml_sb[:], scalar1=0.5, scalar2=None, op0=ALU.mult), reads=[t_oml], writes=[t_gc])
        S.op("dve", lambda: nc.vector.tensor_tensor(out=gc0[:], in0=gc1[:], in1=lb_sb[:], op=ALU.add), reads=[t_gc, t_lb], writes=[t_gc])
        S.op("dve", lambda: nc.vector.tensor_scalar(out=gnc1[:], in0=gc1[:], scalar1=-1.0, scalar2=None, op0=ALU.mult), reads=[t_gc], writes=[t_gc])
        S.op("dve", lambda: nc.vector.tensor_scalar(out=gomc0[:], in0=gc0[:], scalar1=-1.0, scalar2=1.0, op0=ALU.mult, op1=ALU.add), reads=[t_gc], writes=[t_gc])
        A.push()

        def load_weight(dst, t_dst, src, K, N, gain=None, col0=0):
            kcn = K // 128
            stg, t_stg = STG, T_STG
            for cb in range(0, N, 1024):
                w = min(1024, N - cb)
                tb = t_dst.blk(cb)
                for kc in range(kcn):
                    sidx = STGI["i"] % len(stg)
                    STGI["i"] += 1
                    S.dma("sp", stg[sidx][:, 0:w], src[kc * 128:(kc + 1) * 128, col0 + cb:col0 + cb + w], writes=[t_stg[sidx]])
                    e = alt(("dve", "act"))
                    o_ap = dst[:, kc, cb:cb + w]
                    i_ap = stg[sidx][:, 0:w]
                    rd = [t_stg[sidx]]
                    if gain is not None:
                        gt, gk = gain
                        rd.append(gk)
                        g_ap = gt[:, kc:kc + 1]
                        if e == "dve":
                            S.op("dve", lambda o=o_ap, a=i_ap, g=g_ap: nc.vector.tensor_scalar(out=o, in0=a, scalar1=g, scalar2=None, op0=ALU.mult), reads=rd, writes=[tb[kc]])
                        else:
                            S.op("act", lambda o=o_ap, a=i_ap, g=g_ap: nc.scalar.activation(out=o, in_=a, func=AF.Copy, scale=g), reads=rd, writes=[tb[kc]])
                    else:
                        if e == "dve":
                            S.op("dve", lambda o=o_ap, a=i_ap: nc.vector.tensor_copy(out=o, in_=a), reads=rd, writes=[tb[kc]])
                        else:
                            S.op("act", lambda o=o_ap, a=i_ap: nc.scalar.copy(out=o, in_=a), reads=rd, writes=[tb[kc]])

        class Norm:
            def __init__(self, N, ps_idx):
                self.N = N
                self.sq = [A.alloc([128, N], BF16) for _ in range(2)]; self.t_sq = TL(2)
                self.lnv = A.alloc([128, N], F32); self.t_lnv = T()
                self.rstd = A.alloc([128, N], F32); self.t_rstd = T()
                self.ps = ps_idx

            def run(self, src, t_src, dst, t_dst, nfeat=D, rstd_only=False):
                N = self.N
                ps, tps = PS[self.ps], PST[self.ps]
                for kc in range(KC):
                    b = kc % 2
                    S.op("act", lambda kc=kc, b=b: nc.scalar.activation(out=self.sq[b][:], in_=src[:, kc, :], func=AF.Square), reads=[t_src[kc]], writes=[self.t_sq[b]])
                    S.op("pe", lambda kc=kc, b=b: nc.tensor.matmul(ps[:, 0:N], lhsT=ones_bf[:], rhs=self.sq[b][:], start=(kc == 0), stop=(kc == KC - 1)), reads=[t_ones, self.t_sq[b]], writes=[tps])
                S.op("act", lambda: nc.scalar.activation(out=self.lnv[:], in_=ps[:, 0:N], func=AF.Ln, scale=1.0 / nfeat, bias=EPS), reads=[tps], writes=[self.t_lnv])
                S.op("act", lambda: nc.scalar.activation(out=self.rstd[:], in_=self.lnv[:], func=AF.Exp, scale=-0.5), reads=[self.t_lnv], writes=[self.t_rstd])
                if rstd_only:
                    return
                self.apply(src, t_src, dst, t_dst)

            def piece(self, src, t_src, kc):
                N = self.N
                ps, tps = PS[self.ps], PST[self.ps]
                b = kc % 2
                S.op("act", lambda: nc.scalar.activation(out=self.sq[b][:], in_=src[:, kc, :], func=AF.Square), reads=[t_src[kc]], writes=[self.t_sq[b]])
                S.op("pe", lambda: nc.tensor.matmul(ps[:, 0:N], lhsT=ones_bf[:], rhs=self.sq[b][:], start=(kc == 0), stop=(kc == KC - 1)), reads=[t_ones, self.t_sq[b]], writes=[tps])

            def rstd_(self, nfeat=D):
                N = self.N
                ps, tps = PS[self.ps], PST[self.ps]
                S.op("act", lambda: nc.scalar.activation(out=self.lnv[:], in_=ps[:, 0:N], func=AF.Ln, scale=1.0 / nfeat, bias=EPS), reads=[tps], writes=[self.t_lnv])
                S.op("act", lambda: nc.scalar.activation(out=self.rstd[:], in_=self.lnv[:], func=AF.Exp, scale=-0.5), reads=[self.t_lnv], writes=[self.t_rstd])

            def apply(self, src, t_src, dst, t_dst):
                for kc in range(KC):
                    S.op("dve", lambda kc=kc: nc.vector.tensor_tensor(out=dst[:, kc, :], in0=src[:, kc, :], in1=self.rstd[:], op=ALU.mult), reads=[t_src[kc], self.t_rstd], writes=[t_dst[kc]])

        def mm_fm(ps_idx, W, t_W, col0, xn, t_xn, N, kcn=KC, M=128, tok0=0):
            ps, tps = PS[ps_idx], PST[ps_idx]
            for kc in range(kcn):
                S.op("pe", lambda kc=kc: nc.tensor.matmul(ps[0:M, 0:N], lhsT=W[:, kc, col0:col0 + M], rhs=xn[:, kc, tok0:tok0 + N], start=(kc == 0), stop=(kc == kcn - 1)), reads=[t_W.rd(col0, kc), t_xn[kc]], writes=[tps])

        def mm_tm(ps_idx, xn, t_xn, tok0, W, t_W, col0, ncols, kcn=KC):
            ps, tps = PS[ps_idx], PST[ps_idx]
            for kc in range(kcn):
                S.op("pe", lambda kc=kc: nc.tensor.matmul(ps[:, 0:ncols], lhsT=xn[:, kc, tok0:tok0 + 128], rhs=W[:, kc, col0:col0 + ncols], start=(kc == 0), stop=(kc == kcn - 1)), reads=[t_W.rd(col0, kc), t_xn[kc]], writes=[tps])

        xT_v = xT.rearrange("(kc p) t -> p kc t", p=128)

        def phase1():
            A.push()
            gm = gains["mix"]
            Wf = A.alloc([128, KC, D], BF16); t_Wf = WT(KC, D)
            Wi = A.alloc([128, KC, D], BF16); t_Wi = WT(KC, D)
            Wq = A.alloc([128, KC, D], BF16); t_Wq = WT(KC, D)
            Wg = A.alloc([128, KC, D], BF16); t_Wg = WT(KC, D)
            xg = A.alloc([128, KC, G], F32); t_xg = TL(KC)
            for kc in range(KC):
                S.dma("sp", xg[:, kc, :], xT_v[:, kc, 0:G], writes=[t_xg[kc]])
            load_weight(Wi, t_Wi, w_in, D, D, gm, col0=5120)
            load_weight(Wf, t_Wf, w_in, D, D, gm, col0=4096)
            load_weight(Wq, t_Wq, w_in, D, D, gm, col0=3072)
            load_weight(Wg, t_Wg, w_in, D, D, gm, col0=6144)
            xn = A.alloc([128, KC, G], BF16); t_xn = TL(KC)
            nrm = Norm(G, 2)
            S.barrier()
            TMP = {}
            TT_ = {}
            for i_, nm_ in enumerate("ABCDEFGH"):
                TMP[nm_] = [A.alloc([128, G], F32), A.alloc_at(STG_OFF + i_ * 2048, [128, G], F32)]
                TT_[nm_] = TL(2)
            qt = A.alloc([128, 8, 256], BF16); t_qt = TL(8)
            kt = A.alloc([128, 8, 256], BF16); t_kt = TL(8)
            kh = A.alloc([128, 2, G], BF16); t_kh = TL(2)
            khT = A.alloc([128, 8, 4, 128], BF16); t_khT = TL(8)
            vtok = A.alloc([128, 4, D], BF16); t_vtok = TL(4)
            sog = A.alloc([128, 8, 256], BF16); t_sog = TL(8)
            ebl = A.alloc([128, 8, 8], F32); t_ebl = TL(8)
            Sf = A.alloc([128, 8, 128], F32); t_Sf = TL(8)
            Sb = A.alloc([128, 8, 128], BF16); t_Sb = T()
            Abf = A.alloc([128, 512], BF16); t_Abf = T()
            osq = A.alloc([128, 512], BF16); t_osq = T()
            olnv = A.alloc([128, 512], F32); t_olnv = T()
            orstd = A.alloc([128, 512], F32); t_orstd = T()
            otmp = A.alloc([128, 512], F32); t_otmp = T()
            hbuf = A.alloc([128, 8, 256], BF16); t_hbuf = T()
            obuf = A.alloc([128, 4, 512], F32); t_obuf = TL(4)
            ghn, t_ghn = gains["hn"]
            S.op("pool", lambda: nc.gpsimd.memset(Sf[:], 0.0), writes=t_Sf)
            t_PA = TL(2); t_PO = TL(2); t_Abfh = TL(2); t_Sbh = TL(2)
            S.op("pool", lambda: nc.gpsimd.memset(Sb[:], 0.0), writes=t_Sbh)
            P_PROJ = (0, 1)
            P_AH, P_O, P_ST0, P_ST1, P_TR = (3, 3), 4, 5, 6, 2
            pp = {"i": 0}

            def nextp():
                pp["i"] += 1
                return P_PROJ[pp["i"] % 2]

            pend = {"onorm": None}
            sogL = [sog, A.alloc([128, 8, 256], BF16)]
            t_sogL = [t_sog, TL(8)]

            def load_x(g):
                for kc in range(KC):
                    S.dma("sp", xg[:, kc, :], xT_v[:, kc, g * G:(g + 1) * G], writes=[t_xg[kc]])

            nrm.run(xg, t_xg, xn, t_xn)
            for g in range(NG_ALL):
                own = True
                sog, t_sog = sogL[g % 2], t_sogL[g % 2]
                for tt in range(4):
                    for ch in range(2):
                        p = nextp()
                        mm_tm(p, xn, t_xn, tt * 128, Wi, t_Wi, ch * 512, 512)
                        e = alt(("act", "dve"))
                        if e == "act":
                            S.op("act", lambda p=p, tt=tt, ch=ch: nc.scalar.copy(out=vtok[:, tt, ch * 512:(ch + 1) * 512], in_=PS[p][:, :]), reads=[PST[p]], writes=[t_vtok[tt]])
                        else:
                            S.op("dve", lambda p=p, tt=tt, ch=ch: nc.vector.tensor_copy(out=vtok[:, tt, ch * 512:(ch + 1) * 512], in_=PS[p][:, :]), reads=[PST[p]], writes=[t_vtok[tt]])
                def stage_a(h):
                    ps = {}
                    ps["f"] = nextp()
                    mm_fm(ps["f"], Wf, t_Wf, h * 128, xn, t_xn, G)
                    if own:
                        ps["q"] = nextp()
                        mm_fm(ps["q"], Wq, t_Wq, h * 128, xn, t_xn, 256, tok0=256)
                        ps["g"] = 7
                        mm_fm(7, Wg, t_Wg, h * 128, xn, t_xn, 256, tok0=256)
                    return ps

                def stage_b1(h, ps):
                    sl = h % 2
                    tmpA, tmpG, tmpH = TMP["A"][sl], TMP["G"][sl], TMP["H"][sl]
                    t_tmpA, t_tmpG, t_tmpH = TT_["A"][sl], TT_["G"][sl], TT_["H"][sl]
                    pf, pq, pg = ps["f"], ps["q"], ps["g"]
                    S.op("act", lambda: nc.scalar.activation(out=tmpA[:], in_=PS[pf][:, :], func=AF.Tanh, scale=0.5), reads=[PST[pf]], writes=[t_tmpA])
                    S.op("act", lambda: nc.scalar.activation(out=tmpH[:, 0:256], in_=PS[pq][:, 0:256], func=AF.Silu), reads=[PST[pq]], writes=[t_tmpH])
                    S.op("act", lambda: nc.scalar.activation(out=tmpG[:, 0:256], in_=PS[pg][:, 0:256], func=AF.Silu), reads=[PST[pg]], writes=[t_tmpG])

                def stage_b2(hs_):
                    V = {}
                    for h in hs_:
                        sl = h % 2
                        V[h] = ([TMP[k][sl] for k in "ABCDEFGH"], [TT_[k][sl] for k in "ABCDEFGH"])

                    def each(fn):
                        for h in hs_:
                            (tmpA, tmpB, tmpC, tmpD, tmpE, tmpF, tmpG, tmpH), (t_A, t_B, t_C, t_D, t_E, t_F, t_G, t_H) = V[h]
                            fn(h, tmpA, tmpB, tmpC, tmpD, tmpE, tmpF, tmpG, tmpH, t_A, t_B, t_C, t_D, t_E, t_F, t_G, t_H)

                    def st1(h, tmpA, tmpB, tmpC, tmpD, tmpE, tmpF, tmpG, tmpH, t_A, t_B, t_C, t_D, t_E, t_F, t_G, t_H):
                        S.op("dve", lambda: nc.vector.tensor_scalar(out=tmpC[:], in0=tmpA[:], scalar1=gnc1[:, h:h + 1], scalar2=gomc0[:, h:h + 1], op0=ALU.mult, op1=ALU.add), reads=[t_A, t_gc], writes=[t_C])
                        S.op("dve", lambda: nc.vector.tensor_scalar(out=tmpA[:], in0=tmpA[:], scalar1=gc1[:, h:h + 1], scalar2=gc0[:, h:h + 1], op0=ALU.mult, op1=ALU.add), reads=[t_A, t_gc], writes=[t_A])
                        S.op("act", lambda: nc.scalar.activation(out=tmpB[:], in_=tmpA[:], func=AF.Ln), reads=[t_A], writes=[t_B])

                    def st2(h, tmpA, tmpB, tmpC, tmpD, tmpE, tmpF, tmpG, tmpH, t_A, t_B, t_C, t_D, t_E, t_F, t_G, t_H):
                        S.op("dve", lambda: nc.vector.tensor_tensor_scan(out=tmpF[:], data0=scanm[:], data1=tmpB[:], initial=0.0, op0=ALU.mult, op1=ALU.add), reads=[t_scanm, t_B], writes=[t_F])
                        S.op("act", lambda: nc.scalar.activation(out=tmpD[:], in_=tmpF[:], func=AF.Exp, scale=-1.0), reads=[t_F], writes=[t_D])
                        S.op("act", lambda: nc.scalar.activation(out=ebl[:, h, :], in_=tmpF[:].rearrange("p (c t) -> p c t", t=64)[:, :, 63], func=AF.Exp), reads=[t_F], writes=[t_ebl[h]])
                        S.op("act", lambda: nc.scalar.activation(out=tmpE[:, 0:256], in_=tmpF[:, 256:512], func=AF.Exp), reads=[t_F], writes=[t_E])

                    def st3(h, tmpA, tmpB, tmpC, tmpD, tmpE, tmpF, tmpG, tmpH, t_A, t_B, t_C, t_D, t_E, t_F, t_G, t_H):
                        S.op("dve", lambda: nc.vector.tensor_tensor(out=tmpD[:], in0=tmpD[:], in1=tmpC[:], op=ALU.mult), reads=[t_D, t_C], writes=[t_D])
                        S.op("dve", lambda: nc.vector.tensor_copy(out=kt[:, h, :], in_=tmpD[:, 256:512]), reads=[t_D], writes=[t_kt[h]])
                        S.op("dve", lambda: nc.vector.tensor_tensor(out=kh[:, h % 2, :].rearrange("p (c t) -> p c t", t=64), in0=tmpD[:].rearrange("p (c t) -> p c t", t=64),
                                                                  in1=ebl[:, h, :].unsqueeze(2).to_broadcast([128, 8, 64]), op=ALU.mult), reads=[t_D, t_ebl[h]], writes=[t_kh[h % 2]])
                        S.op("dve", lambda: nc.vector.scalar_tensor_tensor(out=qt[:, h, :], in0=tmpH[:, 0:256], scalar=float(128 ** -0.5), in1=tmpE[:, 0:256], op0=ALU.mult, op1=ALU.mult), reads=[t_H, t_E], writes=[t_qt[h]])
                        S.op("dve", lambda sg=sog: nc.vector.tensor_scalar(out=sg[:, h, :], in0=tmpG[:, 0:256], scalar1=ghn[:, h:h + 1], scalar2=None, op0=ALU.mult), reads=[t_G, t_ghn], writes=[t_sog[h]])

                    each(st1)
                    each(st2)
                    each(st3)

                def stage_c(h):
                    for m in range(4):
                        S.op("pe", lambda m=m: nc.tensor.matmul(PS[P_TR][:, m * 128:(m + 1) * 128], lhsT=kh[:, h % 2, m * 128:(m + 1) * 128], rhs=ident_bf[:], start=True, stop=True), reads=[t_kh[h % 2], t_ident], writes=[PST[P_TR]])
                    S.op("act", lambda: nc.scalar.copy(out=khT[:, h, :, :].rearrange("p m d -> p (m d)"), in_=PS[P_TR][:, :]), reads=[PST[P_TR]], writes=[t_khT[h]])

                for hp in range(0, 8, 2):
                    for h in (hp, hp + 1):
                        stage_b1(h, stage_a(h))
                    if hp >= 2:
                        stage_c(hp - 2)
                        stage_c(hp - 1)
                    stage_b2((hp, hp + 1))
                    if pend["onorm"] is not None:
                        pend["onorm"](hp // 2)
                stage_c(6)
                stage_c(7)
                if pend["onorm"] is not None:
                    pend["onorm"](4)
                    pend["onorm"] = None
                if g + 1 < NG_ALL:
                    load_x(g + 1)
                tail = {"f": None}
                for c in range(8):
                    par = c % 2
                    m = c // 2
                    r0 = par * 64
                    ownc = c >= 4
                    co = c - 4
                    if ownc:
                        for half in range(2):
                            pa = P_AH[half]
                            for h in range(half * 4, half * 4 + 4):
                                S.op("pe", lambda h=h, co=co, r0=r0, pa=pa: nc.tensor.matmul(PS[pa][r0:r0 + 64, (h % 4) * 64:(h % 4 + 1) * 64], lhsT=kt[:, h, co * 64:(co + 1) * 64], rhs=qt[:, h, co * 64:(co + 1) * 64], start=True, stop=True),
                                     reads=[t_kt[h], t_qt[h]], writes=[PST[pa]])
                            S.op("dve", lambda r0=r0, half=half, pa=pa: nc.vector.tensor_tensor(out=Abf[r0:r0 + 64, half * 256:(half + 1) * 256].rearrange("p (h t) -> p h t", t=64), in0=PS[pa][r0:r0 + 64, 0:256].rearrange("p (h t) -> p h t", t=64),
                                                                                              in1=tri64[r0:r0 + 64, :].unsqueeze(1).to_broadcast([64, 4, 64]), op=ALU.mult), reads=[PST[pa], t_tri64], writes=[t_Abfh[half]])
                    for half in range(2):
                        hs = range(half * 4, half * 4 + 4)
                        pst = P_ST0 if half == 0 else P_ST1
                        for h in hs:
                            S.op("pe", lambda h=h, m=m, r0=r0, pst=pst: nc.tensor.matmul(PS[pst][:, (h % 4) * 128:(h % 4 + 1) * 128], lhsT=khT[r0:r0 + 64, h, m, :], rhs=vtok[r0:r0 + 64, m, h * 128:(h + 1) * 128], start=True, stop=True),
                                 reads=[t_khT[h], t_vtok[m]], writes=[PST[pst]])
                    for half in range(2):
                        hs = range(half * 4, half * 4 + 4)
                        pst = P_ST0 if half == 0 else P_ST1
                        if ownc:
                            for h in hs:
                                S.op("pe", lambda h=h, m=m, r0=r0: nc.tensor.matmul(PS[P_O][:, h * 64:(h + 1) * 64], lhsT=vtok[r0:r0 + 64, m, h * 128:(h + 1) * 128], rhs=Abf[r0:r0 + 64, h * 64:(h + 1) * 64], start=True, stop=False),
                                     reads=[t_vtok[m], t_Abfh[half]], writes=[t_PO[half]])
                                S.op("pe", lambda h=h, co=co: nc.tensor.matmul(PS[P_O][:, h * 64:(h + 1) * 64], lhsT=Sb[:, h, :], rhs=qt[:, h, co * 64:(co + 1) * 64], start=False, stop=True),
                                     reads=[t_Sbh[half], t_qt[h]], writes=[t_PO[half]])
                        for h in hs:
                            S.op("dve", lambda h=h, c=c, pst=pst: nc.vector.scalar_tensor_tensor(out=Sf[:, h, :], in0=Sf[:, h, :], scalar=ebl[:, h, c:c + 1], in1=PS[pst][:, (h % 4) * 128:(h % 4 + 1) * 128], op0=ALU.mult, op1=ALU.add),
                                 reads=[t_Sf[h], t_ebl[h], PST[pst]], writes=[t_Sf[h]])
                        S.op("act", lambda half=half: nc.scalar.copy(out=Sb[:, half * 4:(half + 1) * 4, :].rearrange("p h d -> p (h d)"), in_=Sf[:, half * 4:(half + 1) * 4, :].rearrange("p h d -> p (h d)")), reads=[t_Sf[h] for h in hs], writes=[t_Sbh[half]])
                    if ownc:
                        S.op("act", lambda co=co: nc.scalar.copy(out=obuf[:, co, :], in_=PS[P_O][:, :]), reads=t_PO, writes=[t_obuf[co]])
                    if g + 1 < NG_ALL:
                        nrm.piece(xg, t_xg, c)
                if g + 1 < NG_ALL:
                    nrm.rstd_()
                    nrm.apply(xg, t_xg, xn, t_xn)
                def onorm(co, g=g, sog=sog, t_sog=t_sog):
                    if co < 4:
                        S.op("act", lambda: nc.scalar.activation(out=osq[:], in_=obuf[:, co, :], func=AF.Square), reads=[t_obuf[co]], writes=[t_osq])
                        S.op("pe", lambda: nc.tensor.matmul(PS[2][:, :], lhsT=ones_bf[:], rhs=osq[:], start=True, stop=True), reads=[t_ones, t_osq], writes=[PST[2]])
                        S.op("act", lambda: nc.scalar.activation(out=olnv[:], in_=PS[2][:, :], func=AF.Ln, scale=1.0 / 128, bias=EPS), reads=[PST[2]], writes=[t_olnv])
                        S.op("act", lambda: nc.scalar.activation(out=orstd[:], in_=olnv[:], func=AF.Exp, scale=-0.5), reads=[t_olnv], writes=[t_orstd])
                        S.op("dve", lambda: nc.vector.tensor_tensor(out=otmp[:], in0=obuf[:, co, :], in1=orstd[:], op=ALU.mult), reads=[t_obuf[co], t_orstd], writes=[t_otmp])
                        S.op("dve", lambda: nc.vector.tensor_tensor(out=hbuf[:, :, co * 64:(co + 1) * 64], in0=otmp[:].rearrange("p (h t) -> p h t", t=64), in1=sog[:, :, co * 64:(co + 1) * 64], op=ALU.mult),
                             reads=[t_otmp] + t_sog, writes=[t_hbuf])
                    else:
                        S.dma("pool", Hsc.rearrange("(h p) t -> p h t", p=128)[:, :, g * 256:(g + 1) * 256], hbuf[:], reads=[t_hbuf], writes=[t_Hsc])
                pend["onorm"] = onorm
            for co in range(5):
                pend["onorm"](co)
            A.pop()

        ksum = A.alloc([128, 8, 32], F32); t_ksum = T()
        A.push()

        class Rope:
            def __init__(self):
                self.posi = A.alloc([128, G], I32); self.t_posi = T()
                self.ang = A.alloc([128, G], F32); self.t_ang = T()
                self.u = A.alloc([128, G], F32); self.t_u = T()
                self.ki = A.alloc([128, G], I32); self.t_ki = T()
                self.cosL = [A.alloc([128, G], F32) for _ in range(2)]; self.t_cosL = TL(2)
                self.sinL = [A.alloc([128, G], F32) for _ in range(2)]; self.t_sinL = TL(2)
                self.tsb = [A.alloc([128, G], F32) for _ in range(2)]; self.t_tsb = TL(2)
                self.a = [A.alloc([128, G], F32) for _ in range(2)]; self.t_a = TL(2)
                self.b = [A.alloc([128, G], F32) for _ in range(2)]; self.t_b = TL(2)

            def tables(self, c0, ts=0, c1=None):
                s = self
                s.cos, s.t_cos, s.sin, s.t_sin = s.cosL[ts], s.t_cosL[ts], s.sinL[ts], s.t_sinL[ts]
                if c1 is None:
                    S.dma("sp", s.posi[:], pos[0:1, c0:c0 + G].partition_broadcast(128), writes=[s.t_posi])
                else:
                    S.dma("sp", s.posi[:, 0:256], pos[0:1, c0:c0 + 256].partition_broadcast(128), writes=[s.t_posi])
                    S.dma("sp", s.posi[:, 256:512], pos[0:1, c1:c1 + 256].partition_broadcast(128), writes=[s.t_posi])
                S.op("dve", lambda: nc.vector.tensor_copy(out=s.ang[:], in_=s.posi[:]), reads=[s.t_posi], writes=[s.t_ang])
                S.op("dve", lambda: nc.vector.tensor_scalar(out=s.ang[:], in0=s.ang[:], scalar1=invf_sb[:, 0:1], scalar2=None, op0=ALU.mult), reads=[s.t_ang, t_invf], writes=[s.t_ang])
                for which in ("sin", "cos"):
                    dst, t_dst = (s.sin, s.t_sin) if which == "sin" else (s.cos, s.t_cos)
                    add = 0.0 if which == "sin" else 0.25
                    S.op("dve", lambda add=add: nc.vector.tensor_scalar(out=s.u[:], in0=s.ang[:], scalar1=float(1 / (2 * np.pi)), scalar2=add, op0=ALU.mult, op1=ALU.add), reads=[s.t_ang], writes=[s.t_u])
                    S.op("dve", lambda: nc.vector.tensor_copy(out=s.ki[:], in_=s.u[:]), reads=[s.t_u], writes=[s.t_ki])
                    S.op("dve", lambda: nc.vector.tensor_copy(out=s.u[:], in_=s.ki[:]), reads=[s.t_ki], writes=[s.t_u])
                    S.op("dve", lambda dst=dst: nc.vector.scalar_tensor_tensor(out=dst[:], in0=s.u[:], scalar=-C1, in1=s.ang[:], op0=ALU.mult, op1=ALU.add), reads=[s.t_u, s.t_ang], writes=[t_dst])
                    S.op("dve", lambda dst=dst: nc.vector.scalar_tensor_tensor(out=dst[:], in0=s.u[:], scalar=-C2, in1=dst[:], op0=ALU.mult, op1=ALU.add), reads=[s.t_u, t_dst], writes=[t_dst])
                    if which == "cos":
                        S.op("dve", lambda dst=dst: nc.vector.tensor_scalar(out=dst[:], in0=dst[:], scalar1=float(np.pi / 2), scalar2=None, op0=ALU.add), reads=[t_dst], writes=[t_dst])
                    S.op("dve", lambda dst=dst: nc.vector.tensor_scalar(out=dst[:], in0=dst[:], scalar1=PI_LO, scalar2=-PI_LO, op0=ALU.min, op1=ALU.max), reads=[t_dst], writes=[t_dst])
                    if which == "sin":
                        S.op("act", lambda dst=dst: nc.scalar.activation(out=dst[:], in_=dst[:], func=AF.Sin, scale=sgn_sb[:, 0:1]), reads=[t_dst, t_sgn], writes=[t_dst])
                    else:
                        S.op("act", lambda dst=dst: nc.scalar.activation(out=dst[:], in_=dst[:], func=AF.Sin), reads=[t_dst], writes=[t_dst])

            def part1(self, p, sl, ts=0):
                s = self
                cos_, t_cos_ = s.cosL[ts], s.t_cosL[ts]
                S.op("act", lambda: nc.scalar.copy(out=s.tsb[sl][:], in_=PS[p][:, :]), reads=[PST[p]], writes=[s.t_tsb[sl]])
                S.op("dve", lambda: nc.vector.tensor_tensor(out=s.a[sl][:], in0=s.tsb[sl][:], in1=cos_[:], op=ALU.mult), reads=[s.t_tsb[sl], t_cos_], writes=[s.t_a[sl]])

            def part2(self, sl, out_f32, t_out, ts=0):
                s = self
                sin_, t_sin_ = s.sinL[ts], s.t_sinL[ts]
                S.op("pe", lambda: nc.tensor.matmul(PS[5][:, :], lhsT=rotm_sb[:], rhs=s.tsb[sl][:], start=True, stop=True), reads=[t_rotm, s.t_tsb[sl]], writes=[PST[5]])
                S.op("dve", lambda: nc.vector.tensor_tensor(out=s.b[sl][:], in0=PS[5][:, :], in1=sin_[:], op=ALU.mult), reads=[PST[5], t_sin_], writes=[s.t_b[sl]])
                S.op("dve", lambda: nc.vector.tensor_tensor(out=out_f32, in0=s.a[sl][:], in1=s.b[sl][:], op=ALU.add), reads=[s.t_a[sl], s.t_b[sl]], writes=[t_out])

        def phase2():
            A.push()
            gm = gains["mix"]
            Wk = A.alloc([128, KC, D], BF16); t_Wk = WT(KC, D)
            Wv = A.alloc([128, KC, D], BF16); t_Wv = WT(KC, D)
            Wga = A.alloc([128, KC, 2 * D], BF16); t_Wga = WT(KC, 2 * D)
            xgL = [A.alloc([128, KC, G], F32) for _ in range(2)]; t_xgL = [TL(KC) for _ in range(2)]
            for kc in range(KC):
                S.dma("sp", xgL[0][:, kc, :], xT_v[:, kc, 0:G], writes=[t_xgL[0][kc]])
            load_weight(Wk, t_Wk, w_in, D, D, gm, col0=1024)
            load_weight(Wv, t_Wv, w_in, D, D, gm, col0=2048)
            load_weight(Wga, t_Wga, w_in, D, 2 * D, gm, col0=7168)
            xnL = [A.alloc([128, KC, G], BF16) for _ in range(2)]; t_xnL = [TL(KC) for _ in range(2)]
            nrmL = [Norm(G, 2) for _ in range(2)]
            rp = Rope()
            kf = [A.alloc([128, G], F32) for _ in range(2)]; t_kf = TL(2)
            kbuf = A.alloc([128, 8, G], BF16); t_kbuf = T()
            vbuf = A.alloc([128, 2, D], BF16); t_vbuf = T()
            gbuf = A.alloc([128, 16, 256], BF16); t_gbuf = T()
            Kv = Ksc.rearrange("(j p) t -> p j t", p=128)
            Gv2 = Gsc.rearrange("(j p) t -> p j t", p=128)
            pp = {"i": 0}
            PB = (0, 1, 3, 4)

            def nextp():
                pp["i"] += 1
                return PB[pp["i"] % 4]

            def pro_early(g):
                gs = g % 2
                if g > 0:
                    for kc in range(KC):
                        S.dma("sp", xgL[gs][:, kc, :], xT_v[:, kc, g * G:(g + 1) * G], writes=[t_xgL[gs][kc]])
                nrmL[gs].run(xgL[gs], t_xgL[gs], None, None, rstd_only=True)

            def pro_late(g):
                gs = g % 2
                nrmL[gs].apply(xgL[gs], t_xgL[gs], xnL[gs], t_xnL[gs])
                rp.tables(g * G, gs)

            n = 0
            pro_early(0)
            pro_late(0)
            for g in range(NG_ALL):
                own = True
                gs = g % 2
                xn, t_xn = xnL[gs], t_xnL[gs]
                if g + 1 < NG_ALL:
                    pro_early(g + 1)
                def k_finish(j, sl, g=g, gs=gs):
                    rp.part2(sl, kf[sl][:], t_kf[sl], gs)
                    for bk in range(2):
                        S.op("act", lambda bk=bk: nc.scalar.activation(out=kbuf[:, j, bk * 256:(bk + 1) * 256], in_=kf[sl][:, bk * 256:(bk + 1) * 256], func=AF.Copy, accum_out=ksum[:, j, 2 * g + bk:2 * g + bk + 1]),
                             reads=[t_kf[sl]], writes=[t_kbuf, t_ksum])
                    if j == 7:
                        S.dma("pool", Kv[:, :, g * G:(g + 1) * G], kbuf[:], reads=[t_kbuf], writes=[t_Ksc])

                prev = None
                for j in range(8):
                    p = nextp()
                    mm_fm(p, Wk, t_Wk, j * 128, xn, t_xn, G)
                    rp.part1(p, j % 2, gs)
                    if prev is not None:
                        k_finish(*prev)
                    prev = (j, j % 2)
                k_finish(*prev)
                if g + 1 < NG_ALL:
                    pro_late(g + 1)
                for tt in range(4):
                    for ch in range(2):
                        p = nextp()
                        mm_tm(p, xn, t_xn, tt * 128, Wv, t_Wv, ch * 512, 512)
                        S.op("act", lambda p=p, tt=tt, ch=ch: nc.scalar.copy(out=vbuf[:, tt % 2, ch * 512:(ch + 1) * 512], in_=PS[p][:, :]), reads=[PST[p]], writes=[t_vbuf])
                    if tt % 2 == 1:
                        r0_ = g * G + (tt - 1) * 128
                        S.dma("pool", Vsc[r0_:r0_ + 256, :].rearrange("(t p) f -> p t f", p=128), vbuf[:], reads=[t_vbuf], writes=[t_Vsc])
                for j in range(16):
                    p = nextp()
                    mm_fm(p, Wga, t_Wga, j * 128, xn, t_xn, 256, tok0=256)
                    S.op("act", lambda p=p, j=j: nc.scalar.activation(out=gbuf[:, j, :], in_=PS[p][:, 0:256], func=AF.Sigmoid), reads=[PST[p]], writes=[t_gbuf])
                    if j % 8 == 7:
                        S.dma("pool", Gv2[:, j - 7:j + 1, g * 256:(g + 1) * 256], gbuf[:, j - 7:j + 1, :], reads=[t_gbuf], writes=[t_Gsc])
            A.pop()

        def phase3a():
            A.push()
            gm = gains["mix"]
            Wq = A.alloc([128, KC, D], BF16); t_Wq = WT(KC, D)
            xgL = [A.alloc([128, KC, G], F32) for _ in range(2)]; t_xgL = [TL(KC) for _ in range(2)]
            for kc in range(KC):
                S.dma("sp", xgL[0][:, kc, 0:256], xT_v[:, kc, 256:512], writes=[t_xgL[0][kc]])
                S.dma("sp", xgL[0][:, kc, 256:512], xT_v[:, kc, 768:1024], writes=[t_xgL[0][kc]])
            load_weight(Wq, t_Wq, w_in, D, D, gm, col0=0)
            xnL = [A.alloc([128, KC, G], BF16) for _ in range(2)]; t_xnL = [TL(KC) for _ in range(2)]
            nrmL = [Norm(G, 2) for _ in range(2)]
            rp = Rope()
            qf = A.alloc([128, 8, G], F32); t_qf = TL(8)
            qbuf = A.alloc([128, 8, G], BF16); t_qbuf = T()
            Qv = Qsc.rearrange("(j p) t -> p j t", p=128)
            gmx = A.alloc([128, 16, 32], F32); t_gmx = T()
            top8 = A.alloc([128, 16, 8], F32); t_top8 = TL(16)
            sel = A.alloc([128, 16, 32], F32); t_sel = T()
            mbL = [A.alloc([128, 16, 32], BF16) for _ in range(2)]; t_mbL = TL(2)
            mbT = A.alloc([32, 16, G], BF16); t_mbT = T()
            pp = {"i": 0}
            PB = (0, 1, 6, 7)

            def nextp():
                pp["i"] += 1
                return PB[pp["i"] % 4]

            P_G, P_T = 3, 4
            ksbd = A.alloc([128, 8, 64], F32); t_ksbd = T()
            S.op("pool", lambda: nc.gpsimd.memset(ksbd[:], 0.0), writes=[t_ksbd])
            S.op("dve", lambda: nc.vector.tensor_copy(out=ksbd[0:64, :, 0:32], in_=ksum[0:64, :, :]), reads=[t_ksum, t_ksbd], writes=[t_ksbd])
            S.op("dve", lambda: nc.vector.tensor_copy(out=ksbd[64:128, :, 32:64], in_=ksum[64:128, :, :]), reads=[t_ksum, t_ksbd], writes=[t_ksbd])

            def pro_early(go):
                gs = go % 2
                ca, cb = (4 * go + 1) * 256, (4 * go + 3) * 256
                for kc in range(KC if go > 0 else 0):
                    S.dma("sp", xgL[gs][:, kc, 0:256], xT_v[:, kc, ca:ca + 256], writes=[t_xgL[gs][kc]])
                    S.dma("sp", xgL[gs][:, kc, 256:512], xT_v[:, kc, cb:cb + 256], writes=[t_xgL[gs][kc]])
                nrmL[gs].run(xgL[gs], t_xgL[gs], None, None, rstd_only=True)

            def pro_late(go):
                gs = go % 2
                ca, cb = (4 * go + 1) * 256, (4 * go + 3) * 256
                nrmL[gs].apply(xgL[gs], t_xgL[gs], xnL[gs], t_xnL[gs])
                rp.tables(ca, gs, c1=cb)

            n = 0
            pro_early(0)
            pro_late(0)
            for go in range(TO // G):
                gs = go % 2
                xn, t_xn = xnL[gs], t_xnL[gs]
                if go + 1 < TO // G:
                    pro_early(go + 1)
                def q_finish(j, sl, go=go, gs=gs):
                    rp.part2(sl, qf[:, j, :], t_qf[j], gs)
                    S.op("act", lambda: nc.scalar.activation(out=qbuf[:, j, :], in_=qf[:, j, :], func=AF.Copy, scale=0.125), reads=[t_qf[j]], writes=[t_qbuf])
                    if j == 7:
                        S.dma("pool", Qv[:, :, go * G:(go + 1) * G], qbuf[:], reads=[t_qbuf], writes=[t_Qsc])

                prev = None
                for j in range(8):
                    p = nextp()
                    mm_fm(p, Wq, t_Wq, j * 128, xn, t_xn, G)
                    rp.part1(p, j % 2, gs)
                    if prev is not None:
                        q_finish(*prev)
                    prev = (j, j % 2)
                q_finish(*prev)
                if go + 1 < TO // G:
                    pro_late(go + 1)
                def gate_tr(qt_, mbs):
                    c0 = qt_ * 128
                    for hq in range(4):
                        for hh in range(4):
                            h = hq * 4 + hh
                            S.op("pe", lambda h=h, hh=hh: nc.tensor.matmul(PS[P_T][0:32, hh * 128:(hh + 1) * 128], lhsT=mbL[mbs][:, h, :], rhs=ident_bf[:], start=True, stop=True), reads=[t_mbL[mbs], t_ident], writes=[PST[P_T]])
                        S.op("act", lambda hq=hq: nc.scalar.copy(out=mbT[:, hq * 4:(hq + 1) * 4, c0:c0 + 128], in_=PS[P_T][0:32, :].rearrange("p (h q) -> p h q", q=128)), reads=[PST[P_T]], writes=[t_mbT])

                for qt_ in range(4):
                    qblk = go * 2 + qt_ // 2
                    c0 = qt_ * 128
                    mbs = qt_ % 2
                    for j in range(8):
                        S.op("pe", lambda j=j, c0=c0: nc.tensor.matmul(PS[P_G][:, j * 64:(j + 1) * 64], lhsT=qf[:, j, c0:c0 + 128], rhs=ksbd[:, j, :], start=True, stop=True),
                             reads=[t_qf[j], t_ksbd], writes=[PST[P_G]])
                    pn = pastneg_sb[:, qblk * 32:(qblk + 1) * 32].unsqueeze(1).to_broadcast([128, 16, 32])
                    S.op("dve", lambda pn=pn: nc.vector.tensor_tensor(out=gmx[:], in0=PS[P_G][:, :].rearrange("p (h n) -> p h n", n=32), in1=pn, op=ALU.add), reads=[PST[P_G], t_pastneg], writes=[t_gmx])
                    for h in range(16):
                        S.op("dve", lambda h=h: nc.vector.max(out=top8[:, h, :], in_=gmx[:, h, :]), reads=[t_gmx], writes=[t_top8[h]])
                    S.op("dve", lambda: nc.vector.tensor_tensor(out=sel[:], in0=gmx[:], in1=top8[:, :, 2:3].to_broadcast([128, 16, 32]), op=ALU.is_ge), reads=[t_gmx] + t_top8, writes=[t_sel])
                    S.op("dve", lambda pn=pn: nc.vector.scalar_tensor_tensor(out=sel[:], in0=sel[:], scalar=NEGM, in1=pn, op0=ALU.mult, op1=ALU.add), reads=[t_sel, t_pastneg], writes=[t_sel])
                    S.op("dve", lambda mbs=mbs: nc.vector.tensor_scalar(out=mbL[mbs][:], in0=sel[:], scalar1=-NEGM, scalar2=-NEGM, op0=ALU.add, op1=ALU.max), reads=[t_sel], writes=[t_mbL[mbs]])
                    if qt_ >= 1:
                        gate_tr(qt_ - 1, (qt_ - 1) % 2)
                gate_tr(3, 1)
                S.dma("pool", Msc.rearrange("h n t -> n h t")[:, :, go * G:(go + 1) * G], mbT[:], reads=[t_mbT], writes=[t_Msc])
            A.pop()

        def phase3b():
            A.push()
            kaug = [A.alloc([96, TT], BF16) for _ in range(2)]; t_kaug = TL(2)
            qaug = [A.alloc([96, TO], BF16) for _ in range(2)]; t_qq = TL(2); t_qm = TL(2)
            vaug = [A.alloc([128, 64, 65], BF16) for _ in range(2)]; t_va = [TL(8) for _ in range(2)]
            ind = A.alloc([32, TT], F32); t_ind = T()
            pt = [A.alloc([128, 512], BF16) for _ in range(3)]; t_pt = TL(3)
            ou = A.alloc([65, 256], F32); t_ou = T()
            rden = A.alloc([65, 256], F32); t_rden = T()
            onesb = A.alloc([65, 64], BF16); t_onesb = T()
            rhi = A.alloc([65, 256], BF16); t_rhi = T()
            rlo = A.alloc([65, 256], BF16); t_rlo = T()
            abuf = [A.alloc([64, TO], BF16) for _ in range(2)]; t_abuf = TL(2)
            S.op("pool", lambda: nc.gpsimd.memset(onesb[:], 1.0), writes=[t_onesb])
            S.op("pool", lambda: nc.gpsimd.memset(ind[:], 1.0), writes=[t_ind])
            S.op("pool", lambda: nc.gpsimd.affine_select(out=ind[:], in_=ind[:], pattern=[[1, TT]], compare_op=ALU.is_ge, fill=0.0, base=0, channel_multiplier=-256), reads=[t_ind], writes=[t_ind])
            S.op("pool", lambda: nc.gpsimd.affine_select(out=ind[:], in_=ind[:], pattern=[[-1, TT]], compare_op=ALU.is_ge, fill=0.0, base=255, channel_multiplier=256), reads=[t_ind], writes=[t_ind])
            for i in range(2):
                S.op("dve", lambda i=i: nc.vector.tensor_copy(out=kaug[i][64:96, :], in_=ind[:]), reads=[t_ind], writes=[t_kaug[i]])
                S.op("pool", lambda i=i: nc.gpsimd.memset(vaug[i][:, :, 64:65], 1.0), writes=t_va[i])
            P_S = (0, 1, 2)
            P_OO = (3, 4)
            P_B = 5
            LOOK = 2
            Vv = Vsc.rearrange("(kt p) (h d) -> p kt h d", p=128, d=64)
            gu = {"u": 0, "o": 0}
            for h in range(16):
                bi = h % 2
                S.dma("sp", kaug[bi][0:64, :], Ksc[h * 64:(h + 1) * 64, :], reads=[t_Ksc], writes=[t_kaug[bi]])
                S.dma("sp", qaug[bi][0:64, :], Qsc[h * 64:(h + 1) * 64, :], reads=[t_Qsc], writes=[t_qq[bi]])
                S.dma("sp", qaug[bi][64:96, :], Msc[h], reads=[t_Msc], writes=[t_qm[bi]])
                for k8 in range(8):
                    S.dma("sp", vaug[bi][:, k8 * 8:(k8 + 1) * 8, 0:64], Vv[:, k8 * 8:(k8 + 1) * 8, h, :], reads=[t_Vsc], writes=[t_va[bi][k8]])
                ka, qa, va = kaug[bi], qaug[bi], vaug[bi]
                tka, tqq, tqm, tva = t_kaug[bi], t_qq[bi], t_qm[bi], t_va[bi]
                units = []
                for qb in range(16):
                    for n in range(2 * qb + 1):
                        units.append(("past", qb, n))
                    units.append(("own", qb, 2 * qb + 1))
                meta = {}
                pending = []

                def emit_qk(i, ka=ka, qa=qa, tka=tka, tqq=tqq, tqm=tqm):
                    kind, qb, n = units[i]
                    u = gu["u"]
                    gu["u"] += 1
                    bank = P_S[u % 3]
                    pti = u % 3
                    meta[i] = (bank, pti)
                    q0 = qb * 256
                    k0 = n * 256
                    if kind == "past":
                        for kc in range(2):
                            S.op("pe", lambda kc=kc, bank=bank, k0=k0, q0=q0: nc.tensor.matmul(PS[bank][:, kc * 256:(kc + 1) * 256], lhsT=ka[0:96, k0 + kc * 128:k0 + (kc + 1) * 128], rhs=qa[0:96, q0:q0 + 256], start=True, stop=True),
                                 reads=[tka, tqq, tqm], writes=[PST[bank]])
                        S.op("act", lambda bank=bank, pti=pti: nc.scalar.activation(out=pt[pti][:], in_=PS[bank][:, :], func=AF.Exp), reads=[PST[bank]], writes=[t_pt[pti]])
                    else:
                        S.op("pe", lambda bank=bank, k0=k0, q0=q0: nc.tensor.matmul(PS[bank][:, 0:256], lhsT=ka[0:64, k0:k0 + 128], rhs=qa[0:64, q0:q0 + 256], start=True, stop=True), reads=[tka, tqq], writes=[PST[bank]])
                        S.op("pe", lambda bank=bank, k0=k0, q0=q0: nc.tensor.matmul(PS[bank][:, 384:512], lhsT=ka[0:64, k0 + 128:k0 + 256], rhs=qa[0:64, q0 + 128:q0 + 256], start=True, stop=True), reads=[tka, tqq], writes=[PST[bank]])
                        S.op("act", lambda bank=bank, pti=pti: nc.scalar.activation(out=pt[pti][:, 0:256], in_=PS[bank][:, 0:256], func=AF.Exp), reads=[PST[bank]], writes=[t_pt[pti]])
                        S.op("act", lambda bank=bank, pti=pti: nc.scalar.activation(out=pt[pti][:, 384:512], in_=PS[bank][:, 384:512], func=AF.Exp), reads=[PST[bank]], writes=[t_pt[pti]])
                        S.op("dve", lambda pti=pti: nc.vector.tensor_tensor(out=pt[pti][:, 0:128], in0=pt[pti][:, 0:128], in1=tri01[:], op=ALU.mult), reads=[t_pt[pti], t_tri], writes=[t_pt[pti]])
                        S.op("dve", lambda pti=pti: nc.vector.tensor_tensor(out=pt[pti][:, 384:512], in0=pt[pti][:, 384:512], in1=tri01[:], op=ALU.mult), reads=[t_pt[pti], t_tri], writes=[t_pt[pti]])

                def emit_pv(i, va=va, tva=tva, bi=bi):
                    kind, qb, n = units[i]
                    bank, pti = meta.pop(i)
                    if kind == "past" and n == 0:
                        gu["o"] += 1
                    po = P_OO[gu["o"] % 2]
                    q0 = qb * 256
                    if kind == "past":
                        for kc in range(2):
                            kt = n * 2 + kc
                            S.op("pe", lambda po=po, pti=pti, kt=kt, kc=kc, st=(n == 0 and kc == 0): nc.tensor.matmul(PS[po][0:65, 0:256], lhsT=va[:, kt, :], rhs=pt[pti][:, kc * 256:(kc + 1) * 256], start=st, stop=False),
                                 reads=[tva[kt // 8], t_pt[pti]], writes=[PST[po]])
                    else:
                        kt = n * 2
                        S.op("pe", lambda po=po, pti=pti, kt=kt: nc.tensor.matmul(PS[po][0:65, 0:256], lhsT=va[:, kt, :], rhs=pt[pti][:, 0:256], start=False, stop=False), reads=[tva[kt // 8], t_pt[pti]], writes=[PST[po]])
                        S.op("pe", lambda po=po, pti=pti, kt=kt: nc.tensor.matmul(PS[po][0:65, 128:256], lhsT=va[:, kt + 1, :], rhs=pt[pti][:, 384:512], start=False, stop=True), reads=[tva[(kt + 1) // 8], t_pt[pti]], writes=[PST[po]])
                        S.op("dve", lambda po=po: nc.vector.tensor_copy(out=ou[:], in_=PS[po][0:65, 0:256]), reads=[PST[po]], writes=[t_ou])
                        S.op("dve", lambda: nc.vector.reciprocal(out=rden[64:65, :], in_=ou[64:65, :]), reads=[t_ou], writes=[t_rden])
                        S.op("dve", lambda: nc.vector.tensor_copy(out=rhi[64:65, :], in_=rden[64:65, :]), reads=[t_rden], writes=[t_rhi])
                        S.op("dve", lambda: nc.vector.tensor_tensor(out=rlo[64:65, :], in0=rden[64:65, :], in1=rhi[64:65, :], op=ALU.subtract), reads=[t_rden, t_rhi], writes=[t_rlo])

                        def part_b(bi=bi, q0=q0):
                            S.op("pe", lambda: nc.tensor.matmul(PS[P_B][0:64, 0:256], lhsT=onesb[64:65, :], rhs=rhi[64:65, :], start=True, stop=False), reads=[t_onesb, t_rhi], writes=[PST[P_B]])
                            S.op("pe", lambda: nc.tensor.matmul(PS[P_B][0:64, 0:256], lhsT=onesb[64:65, :], rhs=rlo[64:65, :], start=False, stop=True), reads=[t_onesb, t_rlo], writes=[PST[P_B]])
                            S.op("dve", lambda bi=bi, q0=q0: nc.vector.tensor_tensor(out=abuf[bi][:, q0:q0 + 256], in0=ou[0:64, :], in1=PS[P_B][0:64, 0:256], op=ALU.mult), reads=[t_ou, PST[P_B]], writes=[t_abuf[bi]])
                        pending.append((i + 3, part_b))

                nu = len(units)
                for i in range(nu + LOOK):
                    if i < nu:
                        emit_qk(i)
                    j = i - LOOK
                    if j >= 0:
                        emit_pv(j)
                    while pending and pending[0][0] <= j:
                        pending.pop(0)[1]()
                while pending:
                    pending.pop(0)[1]()
                S.dma("pool", Asc[h * 64:(h + 1) * 64, :], abuf[bi][:], reads=[t_abuf[bi]], writes=[t_Asc])
            A.pop()

        def phase4a():
            A.push()
            Wa = A.alloc([128, KC, D], BF16); t_Wa = WT(KC, D)
            Wh = A.alloc([128, KC, D], BF16); t_Wh = WT(KC, D)
            Wo = A.alloc([128, KC, D], BF16); t_Wo = WT(KC, D)
            Wqx = A.alloc([128, KC, D], BF16); t_Wqx = WT(KC, D)
            Wox = A.alloc([128, KC, D], BF16); t_Wox = WT(KC, D)
            kxT = A.alloc([128, KC, NMEM], BF16); t_kxT = T()
            vx = A.alloc([128, 2, D], BF16); t_vx = T()
            A.push()
            Wkv = A.alloc([128, KC, 2 * D], BF16); t_Wkv = WT(KC, 2 * D)
            mg_ = A.alloc([128, KC, NMEM], F32); t_mg = TL(KC)
            mn = A.alloc([128, KC, NMEM], BF16); t_mn = TL(KC)
            nm = Norm(NMEM, 2)
            memT_v = memT.rearrange("(kc p) t -> p kc t", p=128)
            for kc in range(KC):
                S.dma("sp", mg_[:, kc, :], memT_v[:, kc, :], writes=[t_mg[kc]])
            load_weight(Wkv, t_Wkv, wkv_x, D, 2 * D, gains["mem"])
            load_weight(Wa, t_Wa, w_bra, D, D)
            load_weight(Wh, t_Wh, w_brh, D, D)
            load_weight(Wo, t_Wo, w_out, D, D)
            load_weight(Wqx, t_Wqx, wq_x, D, D, gains["x"])
            load_weight(Wox, t_Wox, wo_x, D, D)
            nm.run(mg_, t_mg, mn, t_mn)
            for oc in range(8):
                p = oc % 2
                mm_fm(p, Wkv, t_Wkv, oc * 128, mn, t_mn, NMEM)
                S.op("act", lambda p=p, oc=oc: nc.scalar.copy(out=kxT[:, oc, :], in_=PS[p][:, 0:NMEM]), reads=[PST[p]], writes=[t_kxT])
            for mt in range(2):
                for ch in range(2):
                    p = (mt * 2 + ch) % 2
                    mm_tm(p, mn, t_mn, mt * 128, Wkv, t_Wkv, D + ch * 512, 512)
                    S.op("act", lambda p=p, mt=mt, ch=ch: nc.scalar.copy(out=vx[:, mt, ch * 512:(ch + 1) * 512], in_=PS[p][:, :]), reads=[PST[p]], writes=[t_vx])
            A.pop()
            S.barrier()
            ag = A.alloc([128, KC, G], BF16); t_ag = TL(KC)
            hg = A.alloc([128, KC, G], BF16); t_hg = TL(KC)
            gg = [A.alloc([128, 2, G], BF16) for _ in range(2)]; t_gg = TL(2)
            xg = [A.alloc([128, G], F32) for _ in range(2)]; t_xg = TL(2)
            mgd = A.alloc([128, KC, G], BF16); t_mgd = TL(KC)
            tmp = A.alloc([128, G], F32); t_tmp = T()
            tmp2 = A.alloc([128, G], F32); t_tmp2 = T()
            h1 = A.alloc([128, KC, G], F32); t_h1 = TL(KC)
            hn = A.alloc([128, KC, G], BF16); t_hn = TL(KC)
            qx = A.alloc([128, KC, G], BF16); t_qx = TL(KC)
            ptx = [[A.alloc([128, G], BF16) for _ in range(2)] for _ in range(2)]; t_ptx = [TL(2) for _ in range(2)]
            rdx = [A.alloc([128, G], F32) for _ in range(2)]; t_rdx = TL(2)
            ox = A.alloc([128, KC, G], BF16); t_ox = TL(KC)
            nrm = Norm(G, 2)
            Av = Asc.rearrange("(kc p) t -> p kc t", p=128)
            Hv = Hsc.rearrange("(kc p) t -> p kc t", p=128)
            Gv = Gsc.rearrange("(s kc p) t -> p s kc t", p=128, s=2)
            H2v = H2sc.rearrange("(kc p) t -> p kc t", p=128)
            pp = {"i": 0}

            def nextp():
                pp["i"] += 1
                return pp["i"] % 2

            n2 = 0
            n3 = 0
            for go in range(TO // G):
                ca, cb = (4 * go + 1) * 256, (4 * go + 3) * 256
                cs = slice(go * G, (go + 1) * G)
                for kc in range(KC):
                    S.dma("sp", ag[:, kc, :], Av[:, kc, cs], reads=[t_Asc], writes=[t_ag[kc]])
                    S.dma("sp", hg[:, kc, :], Hv[:, kc, cs], reads=[t_Hsc], writes=[t_hg[kc]])
                for oc in range(KC):
                    gi = n3 % 2
                    n3 += 1
                    S.dma("sp", gg[gi][:], Gv[:, :, oc, cs], reads=[t_Gsc], writes=[t_gg[gi]])
                    p = nextp()
                    mm_fm(p, Wa, t_Wa, oc * 128, ag, t_ag, G)
                    S.op("dve", lambda p=p, gi=gi: nc.vector.tensor_tensor(out=tmp[:], in0=PS[p][:, :], in1=gg[gi][:, 0, :], op=ALU.mult), reads=[PST[p], t_gg[gi]], writes=[t_tmp])
                    p = nextp()
                    mm_fm(p, Wh, t_Wh, oc * 128, hg, t_hg, G)
                    S.op("dve", lambda p=p, gi=gi: nc.vector.tensor_tensor(out=tmp2[:], in0=PS[p][:, :], in1=gg[gi][:, 1, :], op=ALU.mult), reads=[PST[p], t_gg[gi]], writes=[t_tmp2])
                    S.op("dve", lambda oc=oc: nc.vector.tensor_tensor(out=mgd[:, oc, :], in0=tmp2[:], in1=tmp[:], op=ALU.add), reads=[t_tmp2, t_tmp], writes=[t_mgd[oc]])
                for oc in range(KC):
                    xi = n3 % 2
                    n3 += 1
                    S.dma("sp", xg[xi][:, 0:256], xT_v[:, oc, ca:ca + 256], writes=[t_xg[xi]])
                    S.dma("sp", xg[xi][:, 256:512], xT_v[:, oc, cb:cb + 256], writes=[t_xg[xi]])
                    p = nextp()
                    mm_fm(p, Wo, t_Wo, oc * 128, mgd, t_mgd, G)
                    S.op("dve", lambda p=p, oc=oc, xi=xi: nc.vector.tensor_tensor(out=h1[:, oc, :], in0=PS[p][:, :], in1=xg[xi][:], op=ALU.add), reads=[PST[p], t_xg[xi]], writes=[t_h1[oc]])
                nrm.run(h1, t_h1, hn, t_hn)
                for oc in range(KC):
                    p = nextp()
                    mm_fm(p, Wqx, t_Wqx, oc * 128, hn, t_hn, G)
                    S.op("act", lambda p=p, oc=oc: nc.scalar.copy(out=qx[:, oc, :], in_=PS[p][:, :]), reads=[PST[p]], writes=[t_qx[oc]])
                def xa_scores(hx):
                    for mt in range(2):
                        p = 3 + mt
                        for dc in range(2):
                            S.op("pe", lambda p=p, mt=mt, dc=dc: nc.tensor.matmul(PS[p][:, :], lhsT=kxT[:, hx * 2 + dc, mt * 128:(mt + 1) * 128], rhs=qx[:, hx * 2 + dc, :], start=(dc == 0), stop=(dc == 1)),
                                 reads=[t_kxT, t_qx[hx * 2 + dc]], writes=[PST[p]])
                        S.op("act", lambda p=p, mt=mt: nc.scalar.activation(out=ptx[hx % 2][mt][:], in_=PS[p][:, :], func=AF.Exp, scale=1.0 / 16), reads=[PST[p]], writes=[t_ptx[hx % 2][mt]])

                def xa_finish(hx):
                    sl_ = hx % 2
                    for mt in range(2):
                        S.op("pe", lambda mt=mt: nc.tensor.matmul(PS[5][:, :], lhsT=ones_bf[:], rhs=ptx[sl_][mt][:], start=(mt == 0), stop=(mt == 1)), reads=[t_ones, t_ptx[sl_][mt]], writes=[PST[5]])
                    S.op("act", lambda: nc.scalar.activation(out=rdx[sl_][:], in_=PS[5][:, :], func=AF.Ln), reads=[PST[5]], writes=[t_rdx[sl_]])
                    S.op("act", lambda: nc.scalar.activation(out=rdx[sl_][:], in_=rdx[sl_][:], func=AF.Exp, scale=-1.0), reads=[t_rdx[sl_]], writes=[t_rdx[sl_]])
                    for dc in range(2):
                        p = 6 + dc
                        for mt in range(2):
                            S.op("pe", lambda p=p, dc=dc, mt=mt: nc.tensor.matmul(PS[p][:, :], lhsT=vx[:, mt, (hx * 2 + dc) * 128:(hx * 2 + dc + 1) * 128], rhs=ptx[sl_][mt][:], start=(mt == 0), stop=(mt == 1)),
                                 reads=[t_vx, t_ptx[sl_][mt]], writes=[PST[p]])
                        S.op("dve", lambda p=p, dc=dc: nc.vector.tensor_tensor(out=ox[:, hx * 2 + dc, :], in0=PS[p][:, :], in1=rdx[sl_][:], op=ALU.mult), reads=[PST[p], t_rdx[sl_]], writes=[t_ox[hx * 2 + dc]])

                for hx in range(4):
                    xa_scores(hx)
                    if hx >= 1:
                        xa_finish(hx - 1)
                xa_finish(3)
                for oc in range(KC):
                    p = nextp()
                    mm_fm(p, Wox, t_Wox, oc * 128, ox, t_ox, G)
                    S.op("dve", lambda p=p, oc=oc: nc.vector.tensor_tensor(out=h1[:, oc, :], in0=PS[p][:, :], in1=h1[:, oc, :], op=ALU.add), reads=[PST[p], t_h1[oc]], writes=[t_h1[oc]])
                    if oc % 4 == 3:
                        S.dma("pool", H2v[:, oc - 3:oc + 1, cs], h1[:, oc - 3:oc + 1, :], reads=t_h1[oc - 3:oc + 1], writes=[t_H2])
            A.pop()

        def phase4b():
            A.push()
            GF = 256
            Wfi = A.alloc([128, KC, 2 * FH], BF16); t_Wfi = WT(KC, 2 * FH)
            Wfo = A.alloc([128, FC, D], BF16); t_Wfo = WT(FC, D)
            hg_ = A.alloc([128, KC, GF], F32); t_hg = TL(KC)
            H2v0 = H2sc.rearrange("(kc p) t -> p kc t", p=128)
            for kc in range(KC):
                S.dma("sp", hg_[:, kc, :], H2v0[:, kc, 0:GF], reads=[t_H2], writes=[t_hg[kc]])
            load_weight(Wfi, t_Wfi, w_fi, D, 2 * FH, gains["ffn"])
            load_weight(Wfo, t_Wfo, w_fo, FH, D)
            hn = A.alloc([128, KC, GF], BF16); t_hn = TL(KC)
            act = A.alloc([128, FC, GF], BF16); t_act = TL(FC)
            sl = A.alloc([128, GF], F32); t_sl = T()
            h3 = A.alloc([128, KC, GF], F32); t_h3 = TL(KC)
            nrm = Norm(GF, 2)
            nrm2 = Norm(GF, 3)
            gfin, t_gfin = gains["fin"]
            H2v = H2sc.rearrange("(kc p) t -> p kc t", p=128)
            Ov = outT.rearrange("(kc p) t -> p kc t", p=128)
            pp = {"i": 0}

            def nextp():
                pp["i"] += 1
                return pp["i"] % 2

            n2 = 0
            for go in range(TO // GF):
                cs = slice(go * GF, (go + 1) * GF)
                for kc in range(KC if go > 0 else 0):
                    S.dma("sp", hg_[:, kc, :], H2v[:, kc, cs], reads=[t_H2], writes=[t_hg[kc]])
                nrm.run(hg_, t_hg, hn, t_hn)
                for j in range(FC):
                    pg = 4 + (j % 2)
                    pu = 6 + (j % 2)
                    mm_fm(pg, Wfi, t_Wfi, (2 * j) * 128, hn, t_hn, GF)
                    mm_fm(pu, Wfi, t_Wfi, (2 * j + 1) * 128, hn, t_hn, GF)
                    S.op("act", lambda pg=pg: nc.scalar.activation(out=sl[:], in_=PS[pg][:, 0:GF], func=AF.Silu), reads=[PST[pg]], writes=[t_sl])
                    S.op("dve", lambda pu=pu, j=j: nc.vector.tensor_tensor(out=act[:, j, :], in0=PS[pu][:, 0:GF], in1=sl[:], op=ALU.mult), reads=[PST[pu], t_sl], writes=[t_act[j]])
                for oc in range(KC):
                    p = nextp()
                    mm_fm(p, Wfo, t_Wfo, oc * 128, act, t_act, GF, kcn=FC)
                    S.op("dve", lambda p=p, oc=oc: nc.vector.tensor_tensor(out=h3[:, oc, :], in0=PS[p][:, 0:GF], in1=hg_[:, oc, :], op=ALU.add), reads=[PST[p], t_hg[oc]], writes=[t_h3[oc]])
                nrm2.run(h3, t_h3, None, None, rstd_only=True)
                for oc in range(KC):
                    S.op("dve", lambda oc=oc: nc.vector.scalar_tensor_tensor(out=h3[:, oc, :], in0=h3[:, oc, :], scalar=gfin[:, oc:oc + 1], in1=nrm2.rstd[:], op0=ALU.mult, op1=ALU.mult),
                         reads=[t_h3[oc], t_gfin, nrm2.t_rstd], writes=[t_h3[oc]])
                S.dma("pool", Ov[:, :, cs], h3[:], reads=t_h3, writes=[t_out])
            A.pop()

        phases = [phase1, phase2, phase3a, phase3b, phase4a, phase4b]
        for i, ph in enumerate(phases):
            if i < stop_after:
                S.barrier()
                ph()
        S.finish([t_Ksc, t_Vsc, t_Qsc, t_Msc, t_Gsc, t_Hsc, t_Asc, t_H2, t_out])
        build_program.n_inst = S.n_inst
    return nc


_NC_CACHE = {}


def make_in_maps(x, mem, positions, norm_mix_g, w_in, hgrn_lb_logits, hgrn_norm_g, w_br_attn,
                 w_br_hgrn, w_out, norm_x_g, norm_mem_g, wq_x, wkv_x, wo_x, norm_ffn_g,
                 w_ffn_in, w_ffn_out, final_norm_g):
    f32 = np.float32

    def gl(g, w=KC):
        return np.ascontiguousarray(np.asarray(g, f32).reshape(w, 128).T)

    inv_freq = np.power(np.float32(10000.0), -np.arange(0, 64, 2, dtype=np.float32) / np.float32(64)).astype(f32)
    invf = np.tile(inv_freq, 4).reshape(128, 1).astype(f32)
    sgn = np.tile(np.concatenate([-np.ones(32, f32), np.ones(32, f32)]), 2).reshape(128, 1)
    rotm = np.zeros((128, 128), f32)
    rotm[np.arange(128), np.arange(128) ^ 32] = 1.0
    lbl = np.asarray(hgrn_lb_logits, f32)
    lbl_l = np.concatenate([gl(lbl[0], 8), gl(lbl[1], 8)], axis=1)
    shared = {
        "invf": invf, "sgn": sgn, "rotm": rotm,
        "w_in": np.ascontiguousarray(w_in[0], dtype=f32), "g_mix": gl(norm_mix_g[0]),
        "lbl": np.ascontiguousarray(lbl_l), "g_hn": gl(hgrn_norm_g[0], 8),
        "w_bra": np.ascontiguousarray(w_br_attn[0], dtype=f32), "w_brh": np.ascontiguousarray(w_br_hgrn[0], dtype=f32),
        "w_out": np.ascontiguousarray(w_out[0], dtype=f32), "g_x": gl(norm_x_g[0]), "g_mem": gl(norm_mem_g[0]),
        "wq_x": np.ascontiguousarray(wq_x[0], dtype=f32), "wkv_x": np.ascontiguousarray(wkv_x[0], dtype=f32),
        "wo_x": np.ascontiguousarray(wo_x[0], dtype=f32), "g_ffn": gl(norm_ffn_g[0]),
        "w_fi": np.ascontiguousarray(np.asarray(w_ffn_in[0], f32).reshape(D, 2, FC, 128).transpose(0, 2, 1, 3).reshape(D, 2 * FH)), "w_fo": np.ascontiguousarray(w_ffn_out[0], dtype=f32),
        "g_fin": gl(final_norm_g),
    }
    in_maps = []
    for c in range(8):
        b, half = c // 2, c % 2
        xs = np.zeros((TT, D), f32)
        ps_ = np.zeros((TT,), np.int32)
        if half == 1:
            xs[:] = np.asarray(x[b], f32)
            ps_[:] = np.asarray(positions[b], np.int32)
        else:
            xs[256:] = np.asarray(x[b, 0:TT - 256], f32)
            ps_[256:] = np.asarray(positions[b, 0:TT - 256], np.int32)
        xT = np.ascontiguousarray(xs.T)
        pos = ps_.reshape(1, TT)
        pastneg = np.full((16, 32), -1e30, f32)
        for qb in range(16):
            lo = 0 if half == 1 else 1
            pastneg[qb, lo:2 * qb + 1] = 0.0
        m = dict(shared)
        m.update({"xT": xT, "pos": pos, "pastneg": pastneg.reshape(1, 512),
                  "memT": np.ascontiguousarray(np.asarray(mem[b], f32).T)})
        in_maps.append(m)
    return in_maps


def kernel(**inputs):
    if "nc" not in _NC_CACHE:
        _NC_CACHE["nc"] = build_program()
    nc = _NC_CACHE["nc"]
    in_maps = make_in_maps(**inputs)
    res = run_bass_kernel_spmd(nc, in_maps, core_ids=list(range(8)))
    out = np.empty((4, 2 * TO, D), np.float32)
    for c in range(8):
        b, half = c // 2, c % 2
        o = res.results[c]["outT"].T.reshape(16, 256, D)
        out[b].reshape(16, 2, 256, D)[:, half] = o
    return out
```

```python
import os
import numpy as np
from contextlib import ExitStack
import concourse.bass as bass
import concourse.mybir as mybir
from concourse.bass_utils import run_bass_kernel_spmd

F32 = mybir.dt.float32
BF16 = mybir.dt.bfloat16
I32 = mybir.dt.int32
ALU = mybir.AluOpType
AF = mybir.ActivationFunctionType
AX = mybir.AxisListType

D = 1024
KC = 8
TP = 4096
TO = 4096
TT = TP + TO
G = 512
NG_ALL = TT // G
NG_PRE = TP // G
FH = 2816
FC = FH // 128
NMEM = 256
EPS = 1e-6
NEGM = 30000.0
PI_LO = 3.1415925
C1 = 6.28125
C2 = float(2 * np.pi - 6.28125)


class T:
    __slots__ = ("writer", "readers")

    def __init__(self):
        self.writer = None
        self.readers = []


def TL(n):
    return [T() for _ in range(n)]


class WT:
    def __init__(self, kcn, ncols):
        self.kcn = kcn
        self.b = {cb: TL(kcn) for cb in range(0, ncols, 1024)}

    def blk(self, cb):
        return self.b[(cb // 1024) * 1024]

    def rd(self, col0, kc):
        return self.b[(col0 // 1024) * 1024][kc]


class Sched:
    def __init__(self, nc, stack, n_dma_sems=24):
        self.nc = nc
        self.eng = {"pe": nc.tensor, "act": nc.scalar, "dve": nc.vector, "pool": nc.gpsimd, "sp": nc.sync}
        self.ops = {k: [] for k in self.eng}
        self.sems = {}
        for k in ("pe", "act", "dve", "pool"):
            self.sems[k] = stack.enter_context(nc.semaphore("s_" + k))
        self.count = {k: 0 for k in ("pe", "act", "dve", "pool")}
        self.waited = {k: {} for k in self.eng}
        self.dma_pool = {}
        for q in ("sp", "pool"):
            lst = []
            for i in range(n_dma_sems):
                key = "d_%s_%d" % (q, i)
                self.sems[key] = stack.enter_context(nc.semaphore(key))
                lst.append([key, 0])
            self.dma_pool[q] = lst
        self.dma_rr = {"sp": 0, "pool": 0}
        self.n_inst = 0

    def _waits(self, e, deps, skip_self=False):
        w = self.waited[e]
        best = {}
        for d in deps:
            if d is None:
                continue
            k, v = d
            if skip_self and k == e:
                continue
            if best.get(k, 0) < v:
                best[k] = v
        out = []
        for k, v in best.items():
            if w.get(k, 0) >= v:
                continue
            w[k] = v
            out.append((k, v))
        return out

    def _emit_waits(self, e, waits):
        eng = self.eng[e]
        for (k, v) in waits:
            sem = self.sems[k]
            self.ops[e].append(lambda eng=eng, sem=sem, v=v: eng.wait_ge(sem, v))
            self.n_inst += 1

    @staticmethod
    def _deps(reads, writes):
        deps = []
        for t in reads:
            deps.append(t.writer)
        for t in writes:
            deps.append(t.writer)
            deps.extend(t.readers)
        return deps

    @staticmethod
    def _update(me, reads, writes):
        for t in writes:
            t.writer = me
            t.readers = []
        for t in reads:
            t.readers.append(me)
            if len(t.readers) > 48:
                best = {}
                for (k, v) in t.readers:
                    if best.get(k, 0) < v:
                        best[k] = v
                t.readers = list(best.items())

    def op(self, e, fn, reads=(), writes=()):
        waits = self._waits(e, self._deps(reads, writes), skip_self=(e == "pe"))
        self._emit_waits(e, waits)
        self.count[e] += 1
        sem = self.sems[e]
        self.ops[e].append(lambda fn=fn, sem=sem: fn().then_inc(sem, 1))
        self.n_inst += 1
        self._update((e, self.count[e]), reads, writes)

    def dma(self, q, out_ap, in_ap, reads=(), writes=()):
        deps = self._deps(reads, writes)
        pool = self.dma_pool[q]
        slot = pool[self.dma_rr[q] % len(pool)]
        self.dma_rr[q] += 1
        key, cnt = slot
        if cnt > 0:
            deps.append((key, cnt))
        self._emit_waits(q, self._waits(q, deps))
        slot[1] = cnt + 16
        sem = self.sems[key]
        eng = self.eng[q]
        self.ops[q].append(lambda eng=eng, o=out_ap, i=in_ap, sem=sem: eng.dma_start(out=o, in_=i).then_inc(sem, 16))
        self.n_inst += 1
        self._update((key, cnt + 16), reads, writes)

    def barrier(self):
        deps = [(k, self.count[k]) for k in self.count if self.count[k] > 0]
        for q in self.dma_pool:
            for key, cnt in self.dma_pool[q]:
                if cnt > 0:
                    deps.append((key, cnt))
        for e in self.eng:
            self._emit_waits(e, self._waits(e, deps, skip_self=(e == "pe")))

    def finish(self, final_tiles):
        deps = []
        for t in final_tiles:
            deps.append(t.writer)
            deps.extend(t.readers)
        self._emit_waits("sp", self._waits("sp", deps))
        nc = self.nc
        ops = self.ops
        with nc.Block() as block:
            @block.tensor
            def _(e):
                for f in ops["pe"]:
                    f()

            @block.scalar
            def _(e):
                for f in ops["act"]:
                    f()

            @block.vector
            def _(e):
                for f in ops["dve"]:
                    f()

            @block.gpsimd
            def _(e):
                for f in ops["pool"]:
                    f()

            @block.sync
            def _(e):
                for f in ops["sp"]:
                    f()


class Arena:
    LO = 20480
    HI = 229376

    def __init__(self, nc):
        self.nc = nc
        self.off = self.LO
        self.n = 0
        self.marks = []

    def alloc(self, shape, dt):
        size = int(np.prod(shape[1:])) * mybir.dt.size(dt)
        size = (size + 63) // 64 * 64
        assert self.off + size <= self.HI, "SBUF overflow: need %d at %d" % (size, self.off)
        self.n += 1
        t = self.nc.alloc_sbuf_tensor_at("sb%d" % self.n, list(shape), dt, offset=self.off)
        self.off += size
        return t

    def alloc_at(self, off, shape, dt):
        self.n += 1
        return self.nc.alloc_sbuf_tensor_at("sb%d" % self.n, list(shape), dt, offset=off)

    def push(self):
        self.marks.append(self.off)

    def pop(self):
        self.off = self.marks.pop()


def build_program(debug=False, stop_after=99):
    nc = bass.Bass("TRN2", target_bir_lowering=False)

    def din(name, shape, dt=F32):
        return nc.dram_tensor(name, list(shape), dt, kind="ExternalInput").ap()

    skind = "ExternalOutput" if debug else "Internal"

    def dscr(name, shape, dt):
        return nc.dram_tensor(name, list(shape), dt, kind=skind).ap()

    xT = din("xT", [D, TT])
    pos = din("pos", [1, TT], I32)
    pastneg = din("pastneg", [1, 16 * 32])
    invf = din("invf", [128, 1])
    sgn = din("sgn", [128, 1])
    rotm = din("rotm", [128, 128])
    memT = din("memT", [D, NMEM])
    w_in = din("w_in", [D, 9216])
    g_mix = din("g_mix", [128, KC])
    lbl = din("lbl", [128, 2 * 8])
    g_hn = din("g_hn", [128, 8])
    w_bra = din("w_bra", [D, D])
    w_brh = din("w_brh", [D, D])
    w_out = din("w_out", [D, D])
    g_x = din("g_x", [128, KC])
    g_mem = din("g_mem", [128, KC])
    wq_x = din("wq_x", [D, D])
    wkv_x = din("wkv_x", [D, 2 * D])
    wo_x = din("wo_x", [D, D])
    g_ffn = din("g_ffn", [128, KC])
    w_fi = din("w_fi", [D, 2 * FH])
    w_fo = din("w_fo", [FH, D])
    g_fin = din("g_fin", [128, KC])
    outT = nc.dram_tensor("outT", [D, TO], F32, kind="ExternalOutput").ap()

    Ksc = dscr("Ksc", [D, TT], BF16)
    Vsc = dscr("Vsc", [TT, D], BF16)
    Qsc = dscr("Qsc", [D, TO], BF16)
    Msc = dscr("Msc", [16, 32, TO], BF16)
    Gsc = dscr("Gsc", [2 * D, TO], BF16)
    Hsc = dscr("Hsc", [D, TO], BF16)
    Asc = dscr("Asc", [D, TO], BF16)
    H2sc = dscr("H2sc", [D, TO], F32)

    t_Ksc, t_Vsc, t_Qsc, t_Msc, t_Gsc, t_Hsc, t_Asc, t_H2 = TL(8)
    t_out = T()

    with ExitStack() as st:
        S = Sched(nc, st)
        A = Arena(nc)
        PS = [st.enter_context(nc.psum_tensor("psb%d" % i, [128, 512], F32)) for i in range(8)]
        PST = TL(8)

        rr = {"n": 0}

        def alt(engs=("dve", "pool")):
            rr["n"] += 1
            return engs[rr["n"] % len(engs)]

        def E(e):
            return S.eng[e]

        ones_bf = A.alloc([128, 128], BF16); t_ones = T()
        ident_bf = A.alloc([128, 128], BF16); t_ident = T()
        ident_f = A.alloc([128, 128], F32); t_identf = T()
        tri01 = A.alloc([128, 128], BF16); t_tri = T()
        tri_f = A.alloc([128, 128], F32)
        tri64 = A.alloc([128, 64], F32); t_tri64 = T()
        scanm = A.alloc([128, G], F32); t_scanm = T()
        invf_sb = A.alloc([128, 1], F32); t_invf = T()
        sgn_sb = A.alloc([128, 1], F32); t_sgn = T()
        pastneg_sb = A.alloc([128, 512], F32); t_pastneg = T()
        gains = {}
        for nm, ap_, w in (("mix", g_mix, KC), ("x", g_x, KC), ("mem", g_mem, KC), ("ffn", g_ffn, KC), ("fin", g_fin, KC), ("hn", g_hn, 8), ("lbl", lbl, 16)):
            tt = A.alloc([128, w], F32)
            tk = T()
            S.dma("sp", tt[:], ap_, writes=[tk])
            gains[nm] = (tt, tk)
        rotm_sb = A.alloc([128, 128], F32); t_rotm = T()
        S.dma("sp", rotm_sb[:], rotm, writes=[t_rotm])
        S.dma("sp", invf_sb[:], invf, writes=[t_invf])
        S.dma("sp", sgn_sb[:], sgn, writes=[t_sgn])
        S.dma("sp", pastneg_sb[:], pastneg.partition_broadcast(128), writes=[t_pastneg])

        S.op("pool", lambda: nc.gpsimd.memset(ident_f[:], 1.0), writes=[t_identf])
        S.op("pool", lambda: nc.gpsimd.affine_select(out=ident_f[:], in_=ident_f[:], pattern=[[-1, 128]], compare_op=ALU.is_equal, fill=0.0, base=0, channel_multiplier=1), reads=[t_identf], writes=[t_identf])
        S.op("dve", lambda: nc.vector.tensor_copy(out=ident_bf[:], in_=ident_f[:]), reads=[t_identf], writes=[t_ident])
        S.op("pool", lambda: nc.gpsimd.memset(ones_bf[:], 1.0), writes=[t_ones])
        t_trif = T()
        S.op("pool", lambda: nc.gpsimd.memset(tri_f[:], 1.0), writes=[t_trif])
        S.op("pool", lambda: nc.gpsimd.affine_select(out=tri_f[:], in_=tri_f[:], pattern=[[1, 128]], compare_op=ALU.is_ge, fill=0.0, base=0, channel_multiplier=-1), reads=[t_trif], writes=[t_trif])
        S.op("dve", lambda: nc.vector.tensor_copy(out=tri01[:], in_=tri_f[:]), reads=[t_trif], writes=[t_tri])
        S.op("dve", lambda: nc.vector.tensor_copy(out=tri64[0:64, :], in_=tri_f[0:64, 0:64]), reads=[t_trif], writes=[t_tri64])
        S.op("dve", lambda: nc.vector.tensor_copy(out=tri64[64:128, :], in_=tri_f[64:128, 64:128]), reads=[t_trif], writes=[t_tri64])
        S.op("pool", lambda: nc.gpsimd.memset(scanm[:], 1.0), writes=[t_scanm])
        S.op("pool", lambda: nc.gpsimd.memset(scanm[:].rearrange("p (c t) -> p c t", t=64)[:, :, 0:1], 0.0), writes=[t_scanm])

        lbt, t_lbl = gains["lbl"]
        lb_sb = A.alloc([128, 8], F32); t_lb = T()
        oml_sb = A.alloc([128, 8], F32); t_oml = T()
        S.op("dve", lambda: nc.vector.tensor_tensor(out=lb_sb[:], in0=lbt[:, 0:8], in1=lbt[:, 8:16], op=ALU.subtract), reads=[t_lbl], writes=[t_lb])
        S.op("act", lambda: nc.scalar.activation(out=lb_sb[:], in_=lb_sb[:], func=AF.Sigmoid), reads=[t_lb], writes=[t_lb])
        S.op("dve", lambda: nc.vector.tensor_scalar(out=oml_sb[:], in0=lb_sb[:], scalar1=-1.0, scalar2=1.0, op0=ALU.mult, op1=ALU.add), reads=[t_lb], writes=[t_oml])

        STG_OFF = A.off
        STG = [A.alloc([128, 1024], F32) for _ in range(4)]
        T_STG = TL(4)
        STGI = {"i": 0}
        noml_sb = A.alloc([128, 8], F32); t_noml = T()
        S.op("dve", lambda: nc.vector.tensor_scalar(out=noml_sb[:], in0=oml_sb[:], scalar1=-1.0, scalar2=None, op0=ALU.mult), reads=[t_oml], writes=[t_noml])
        gc1 = A.alloc([128, 8], F32); gc0 = A.alloc([128, 8], F32); gnc1 = A.alloc([128, 8], F32); gomc0 = A.alloc([128, 8], F32)
        t_gc = T()
        S.op("dve", lambda: nc.vector.tensor_scalar(out=gc1[:], in0=oml_sb[:], scalar1=0.5, scalar2=None, op0=ALU.mult), reads=[t_oml], writes=[t_gc])
        S.op("dve", lambda: nc.vector.tensor_tensor(out=gc0[:], in0=gc1[:], in1=lb_sb[:], op=ALU.add), reads=[t_gc, t_lb], writes=[t_gc])
        S.op("dve", lambda: nc.vector.tensor_scalar(out=gnc1[:], in0=gc1[:], scalar1=-1.0, scalar2=None, op0=ALU.mult), reads=[t_gc], writes=[t_gc])
        S.op("dve", lambda: nc.vector.tensor_scalar(out=gomc0[:], in0=gc0[:], scalar1=-1.0, scalar2=1.0, op0=ALU.mult, op1=ALU.add), reads=[t_gc], writes=[t_gc])
        A.push()

        def load_weight(dst, t_dst, src, K, N, gain=None, col0=0):
            kcn = K // 128
            stg, t_stg = STG, T_STG
            for cb in range(0, N, 1024):
                w = min(1024, N - cb)
                tb = t_dst.blk(cb)
                for kc in range(kcn):
                    sidx = STGI["i"] % len(stg)
                    STGI["i"] += 1
                    S.dma("sp", stg[sidx][:, 0:w], src[kc * 128:(kc + 1) * 128, col0 + cb:col0 + cb + w], writes=[t_stg[sidx]])
                    e = alt(("dve", "act"))
                    o_ap = dst[:, kc, cb:cb + w]
                    i_ap = stg[sidx][:, 0:w]
                    rd = [t_stg[sidx]]
                    if gain is not None:
                        gt, gk = gain
                        rd.append(gk)
                        g_ap = gt[:, kc:kc + 1]
                        if e == "dve":
                            S.op("dve", lambda o=o_ap, a=i_ap, g=g_ap: nc.vector.tensor_scalar(out=o, in0=a, scalar1=g, scalar2=None, op0=ALU.mult), reads=rd, writes=[tb[kc]])
                        else:
                            S.op("act", lambda o=o_ap, a=i_ap, g=g_ap: nc.scalar.activation(out=o, in_=a, func=AF.Copy, scale=g), reads=rd, writes=[tb[kc]])
                    else:
                        if e == "dve":
                            S.op("dve", lambda o=o_ap, a=i_ap: nc.vector.tensor_copy(out=o, in_=a), reads=rd, writes=[tb[kc]])
                        else:
                            S.op("act", lambda o=o_ap, a=i_ap: nc.scalar.copy(out=o, in_=a), reads=rd, writes=[tb[kc]])

        class Norm:
            def __init__(self, N, ps_idx):
                self.N = N
                self.sq = [A.alloc([128, N], BF16) for _ in range(2)]; self.t_sq = TL(2)
                self.lnv = A.alloc([128, N], F32); self.t_lnv = T()
                self.rstd = A.alloc([128, N], F32); self.t_rstd = T()
                self.ps = ps_idx

            def run(self, src, t_src, dst, t_dst, nfeat=D, rstd_only=False):
                N = self.N
                ps, tps = PS[self.ps], PST[self.ps]
                for kc in range(KC):
                    b = kc % 2
                    S.op("act", lambda kc=kc, b=b: nc.scalar.activation(out=self.sq[b][:], in_=src[:, kc, :], func=AF.Square), reads=[t_src[kc]], writes=[self.t_sq[b]])
                    S.op("pe", lambda kc=kc, b=b: nc.tensor.matmul(ps[:, 0:N], lhsT=ones_bf[:], rhs=self.sq[b][:], start=(kc == 0), stop=(kc == KC - 1)), reads=[t_ones, self.t_sq[b]], writes=[tps])
                S.op("act", lambda: nc.scalar.activation(out=self.lnv[:], in_=ps[:, 0:N], func=AF.Ln, scale=1.0 / nfeat, bias=EPS), reads=[tps], writes=[self.t_lnv])
                S.op("act", lambda: nc.scalar.activation(out=self.rstd[:], in_=self.lnv[:], func=AF.Exp, scale=-0.5), reads=[self.t_lnv], writes=[self.t_rstd])
                if rstd_only:
                    return
                self.apply(src, t_src, dst, t_dst)

            def piece(self, src, t_src, kc):
                N = self.N
                ps, tps = PS[self.ps], PST[self.ps]
                b = kc % 2
                S.op("act", lambda: nc.scalar.activation(out=self.sq[b][:], in_=src[:, kc, :], func=AF.Square), reads=[t_src[kc]], writes=[self.t_sq[b]])
                S.op("pe", lambda: nc.tensor.matmul(ps[:, 0:N], lhsT=ones_bf[:], rhs=self.sq[b][:], start=(kc == 0), stop=(kc == KC - 1)), reads=[t_ones, self.t_sq[b]], writes=[tps])

            def rstd_(self, nfeat=D):
                N = self.N
                ps, tps = PS[self.ps], PST[self.ps]
                S.op("act", lambda: nc.scalar.activation(out=self.lnv[:], in_=ps[:, 0:N], func=AF.Ln, scale=1.0 / nfeat, bias=EPS), reads=[tps], writes=[self.t_lnv])
                S.op("act", lambda: nc.scalar.activation(out=self.rstd[:], in_=self.lnv[:], func=AF.Exp, scale=-0.5), reads=[self.t_lnv], writes=[self.t_rstd])

            def apply(self, src, t_src, dst, t_dst):
                for kc in range(KC):
                    S.op("dve", lambda kc=kc: nc.vector.tensor_tensor(out=dst[:, kc, :], in0=src[:, kc, :], in1=self.rstd[:], op=ALU.mult), reads=[t_src[kc], self.t_rstd], writes=[t_dst[kc]])

        def mm_fm(ps_idx, W, t_W, col0, xn, t_xn, N, kcn=KC, M=128, tok0=0):
            ps, tps = PS[ps_idx], PST[ps_idx]
            for kc in range(kcn):
                S.op("pe", lambda kc=kc: nc.tensor.matmul(ps[0:M, 0:N], lhsT=W[:, kc, col0:col0 + M], rhs=xn[:, kc, tok0:tok0 + N], start=(kc == 0), stop=(kc == kcn - 1)), reads=[t_W.rd(col0, kc), t_xn[kc]], writes=[tps])

        def mm_tm(ps_idx, xn, t_xn, tok0, W, t_W, col0, ncols, kcn=KC):
            ps, tps = PS[ps_idx], PST[ps_idx]
            for kc in range(kcn):
                S.op("pe", lambda kc=kc: nc.tensor.matmul(ps[:, 0:ncols], lhsT=xn[:, kc, tok0:tok0 + 128], rhs=W[:, kc, col0:col0 + ncols], start=(kc == 0), stop=(kc == kcn - 1)), reads=[t_W.rd(col0, kc), t_xn[kc]], writes=[tps])

        xT_v = xT.rearrange("(kc p) t -> p kc t", p=128)

        def phase1():
            A.push()
            gm = gains["mix"]
            Wf = A.alloc([128, KC, D], BF16); t_Wf = WT(KC, D)
            Wi = A.alloc([128, KC, D], BF16); t_Wi = WT(KC, D)
            Wq = A.alloc([128, KC, D], BF16); t_Wq = WT(KC, D)
            Wg = A.alloc([128, KC, D], BF16); t_Wg = WT(KC, D)
            load_weight(Wf, t_Wf, w_in, D, D, gm, col0=4096)
            load_weight(Wi, t_Wi, w_in, D, D, gm, col0=5120)
            load_weight(Wq, t_Wq, w_in, D, D, gm, col0=3072)
            load_weight(Wg, t_Wg, w_in, D, D, gm, col0=6144)
            xg = A.alloc([128, KC, G], F32); t_xg = TL(KC)
            xn = A.alloc([128, KC, G], BF16); t_xn = TL(KC)
            nrm = Norm(G, 2)
            S.barrier()
            TMP = {}
            TT_ = {}
            for i_, nm_ in enumerate("ABCDEFGH"):
                TMP[nm_] = [A.alloc([128, G], F32), A.alloc_at(STG_OFF + i_ * 2048, [128, G], F32)]
                TT_[nm_] = TL(2)
            qt = A.alloc([128, 8, 256], BF16); t_qt = TL(8)
            kt = A.alloc([128, 8, 256], BF16); t_kt = TL(8)
            kh = A.alloc([128, 2, G], BF16); t_kh = TL(2)
            khT = A.alloc([128, 8, 4, 128], BF16); t_khT = TL(8)
            vtok = A.alloc([128, 4, D], BF16); t_vtok = TL(4)
            sog = A.alloc([128, 8, 256], BF16); t_sog = TL(8)
            ebl = A.alloc([128, 8, 8], F32); t_ebl = TL(8)
            Sf = A.alloc([128, 8, 128], F32); t_Sf = TL(8)
            Sb = A.alloc([128, 8, 128], BF16); t_Sb = T()
            Abf = A.alloc([128, 512], BF16); t_Abf = T()
            osq = A.alloc([128, 512], BF16); t_osq = T()
            olnv = A.alloc([128, 512], F32); t_olnv = T()
            orstd = A.alloc([128, 512], F32); t_orstd = T()
            otmp = A.alloc([128, 512], F32); t_otmp = T()
            hbuf = A.alloc([128, 8, 256], BF16); t_hbuf = T()
            obuf = A.alloc([128, 4, 512], F32); t_obuf = TL(4)
            ghn, t_ghn = gains["hn"]
            S.op("pool", lambda: nc.gpsimd.memset(Sf[:], 0.0), writes=t_Sf)
            t_PA = TL(2); t_PO = TL(2); t_Abfh = TL(2); t_Sbh = TL(2)
            S.op("pool", lambda: nc.gpsimd.memset(Sb[:], 0.0), writes=t_Sbh)
            P_PROJ = (0, 1)
            P_AH, P_O, P_ST0, P_ST1, P_TR = (3, 3), 4, 5, 6, 2
            pp = {"i": 0}

            def nextp():
                pp["i"] += 1
                return P_PROJ[pp["i"] % 2]

            pend = {"onorm": None}
            sogL = [sog, A.alloc([128, 8, 256], BF16)]
            t_sogL = [t_sog, TL(8)]

            def load_x(g):
                for kc in range(KC):
                    S.dma("sp", xg[:, kc, :], xT_v[:, kc, g * G:(g + 1) * G], writes=[t_xg[kc]])

            load_x(0)
            nrm.run(xg, t_xg, xn, t_xn)
            for g in range(NG_ALL):
                own = True
                sog, t_sog = sogL[g % 2], t_sogL[g % 2]
                for tt in range(4):
                    for ch in range(2):
                        p = nextp()
                        mm_tm(p, xn, t_xn, tt * 128, Wi, t_Wi, ch * 512, 512)
                        e = alt(("act", "dve"))
                        if e == "act":
                            S.op("act", lambda p=p, tt=tt, ch=ch: nc.scalar.copy(out=vtok[:, tt, ch * 512:(ch + 1) * 512], in_=PS[p][:, :]), reads=[PST[p]], writes=[t_vtok[tt]])
                        else:
                            S.op("dve", lambda p=p, tt=tt, ch=ch: nc.vector.tensor_copy(out=vtok[:, tt, ch * 512:(ch + 1) * 512], in_=PS[p][:, :]), reads=[PST[p]], writes=[t_vtok[tt]])
                def stage_a(h):
                    ps = {}
                    ps["f"] = nextp()
                    mm_fm(ps["f"], Wf, t_Wf, h * 128, xn, t_xn, G)
                    if own:
                        ps["q"] = nextp()
                        mm_fm(ps["q"], Wq, t_Wq, h * 128, xn, t_xn, 256, tok0=256)
                        ps["g"] = 7
                        mm_fm(7, Wg, t_Wg, h * 128, xn, t_xn, 256, tok0=256)
                    return ps

                def stage_b1(h, ps):
                    sl = h % 2
                    tmpA, tmpG, tmpH = TMP["A"][sl], TMP["G"][sl], TMP["H"][sl]
                    t_tmpA, t_tmpG, t_tmpH = TT_["A"][sl], TT_["G"][sl], TT_["H"][sl]
                    pf, pq, pg = ps["f"], ps["q"], ps["g"]
                    S.op("act", lambda: nc.scalar.activation(out=tmpA[:], in_=PS[pf][:, :], func=AF.Tanh, scale=0.5), reads=[PST[pf]], writes=[t_tmpA])
                    S.op("act", lambda: nc.scalar.activation(out=tmpH[:, 0:256], in_=PS[pq][:, 0:256], func=AF.Silu), reads=[PST[pq]], writes=[t_tmpH])
                    S.op("act", lambda: nc.scalar.activation(out=tmpG[:, 0:256], in_=PS[pg][:, 0:256], func=AF.Silu), reads=[PST[pg]], writes=[t_tmpG])

                def stage_b2(hs_):
                    V = {}
                    for h in hs_:
                        sl = h % 2
                        V[h] = ([TMP[k][sl] for k in "ABCDEFGH"], [TT_[k][sl] for k in "ABCDEFGH"])

                    def each(fn):
                        for h in hs_:
                            (tmpA, tmpB, tmpC, tmpD, tmpE, tmpF, tmpG, tmpH), (t_A, t_B, t_C, t_D, t_E, t_F, t_G, t_H) = V[h]
                            fn(h, tmpA, tmpB, tmpC, tmpD, tmpE, tmpF, tmpG, tmpH, t_A, t_B, t_C, t_D, t_E, t_F, t_G, t_H)

                    def st1(h, tmpA, tmpB, tmpC, tmpD, tmpE, tmpF, tmpG, tmpH, t_A, t_B, t_C, t_D, t_E, t_F, t_G, t_H):
                        S.op("dve", lambda: nc.vector.tensor_scalar(out=tmpC[:], in0=tmpA[:], scalar1=gnc1[:, h:h + 1], scalar2=gomc0[:, h:h + 1], op0=ALU.mult, op1=ALU.add), reads=[t_A, t_gc], writes=[t_C])
                        S.op("dve", lambda: nc.vector.tensor_scalar(out=tmpA[:], in0=tmpA[:], scalar1=gc1[:, h:h + 1], scalar2=gc0[:, h:h + 1], op0=ALU.mult, op1=ALU.add), reads=[t_A, t_gc], writes=[t_A])
                        S.op("act", lambda: nc.scalar.activation(out=tmpB[:], in_=tmpA[:], func=AF.Ln), reads=[t_A], writes=[t_B])

                    def st2(h, tmpA, tmpB, tmpC, tmpD, tmpE, tmpF, tmpG, tmpH, t_A, t_B, t_C, t_D, t_E, t_F, t_G, t_H):
                        S.op("dve", lambda: nc.vector.tensor_tensor_scan(out=tmpF[:], data0=scanm[:], data1=tmpB[:], initial=0.0, op0=ALU.mult, op1=ALU.add), reads=[t_scanm, t_B], writes=[t_F])
                        S.op("act", lambda: nc.scalar.activation(out=tmpD[:], in_=tmpF[:], func=AF.Exp, scale=-1.0), reads=[t_F], writes=[t_D])
                        S.op("act", lambda: nc.scalar.activation(out=ebl[:, h, :], in_=tmpF[:].rearrange("p (c t) -> p c t", t=64)[:, :, 63], func=AF.Exp), reads=[t_F], writes=[t_ebl[h]])
                        S.op("act", lambda: nc.scalar.activation(out=tmpE[:, 0:256], in_=tmpF[:, 256:512], func=AF.Exp), reads=[t_F], writes=[t_E])

                    def st3(h, tmpA, tmpB, tmpC, tmpD, tmpE, tmpF, tmpG, tmpH, t_A, t_B, t_C, t_D, t_E, t_F, t_G, t_H):
                        S.op("dve", lambda: nc.vector.tensor_tensor(out=tmpD[:], in0=tmpD[:], in1=tmpC[:], op=ALU.mult), reads=[t_D, t_C], writes=[t_D])
                        S.op("dve", lambda: nc.vector.tensor_copy(out=kt[:, h, :], in_=tmpD[:, 256:512]), reads=[t_D], writes=[t_kt[h]])
                        S.op("dve", lambda: nc.vector.tensor_tensor(out=kh[:, h % 2, :].rearrange("p (c t) -> p c t", t=64), in0=tmpD[:].rearrange("p (c t) -> p c t", t=64),
                                                                  in1=ebl[:, h, :].unsqueeze(2).to_broadcast([128, 8, 64]), op=ALU.mult), reads=[t_D, t_ebl[h]], writes=[t_kh[h % 2]])
                        S.op("dve", lambda: nc.vector.scalar_tensor_tensor(out=qt[:, h, :], in0=tmpH[:, 0:256], scalar=float(128 ** -0.5), in1=tmpE[:, 0:256], op0=ALU.mult, op1=ALU.mult), reads=[t_H, t_E], writes=[t_qt[h]])
                        S.op("dve", lambda sg=sog: nc.vector.tensor_scalar(out=sg[:, h, :], in0=tmpG[:, 0:256], scalar1=ghn[:, h:h + 1], scalar2=None, op0=ALU.mult), reads=[t_G, t_ghn], writes=[t_sog[h]])

                    each(st1)
                    each(st2)
                    each(st3)

                def stage_c(h):
                    for m in range(4):
                        S.op("pe", lambda m=m: nc.tensor.matmul(PS[P_TR][:, m * 128:(m + 1) * 128], lhsT=kh[:, h % 2, m * 128:(m + 1) * 128], rhs=ident_bf[:], start=True, stop=True), reads=[t_kh[h % 2], t_ident], writes=[PST[P_TR]])
                    S.op("act", lambda: nc.scalar.copy(out=khT[:, h, :, :].rearrange("p m d -> p (m d)"), in_=PS[P_TR][:, :]), reads=[PST[P_TR]], writes=[t_khT[h]])

                for hp in range(0, 8, 2):
                    for h in (hp, hp + 1):
                        stage_b1(h, stage_a(h))
                    if hp >= 2:
                        stage_c(hp - 2)
                        stage_c(hp - 1)
                    stage_b2((hp, hp + 1))
                    if pend["onorm"] is not None:
                        pend["onorm"](hp // 2)
                stage_c(6)
                stage_c(7)
                if pend["onorm"] is not None:
                    pend["onorm"](4)
                    pend["onorm"] = None
                if g + 1 < NG_ALL:
                    load_x(g + 1)
                tail = {"f": None}
                for c in range(8):
                    par = c % 2
                    m = c // 2
                    r0 = par * 64
                    ownc = c >= 4
                    co = c - 4
                    if ownc:
                        for half in range(2):
                            pa = P_AH[half]
                            for h in range(half * 4, half * 4 + 4):
                                S.op("pe", lambda h=h, co=co, r0=r0, pa=pa: nc.tensor.matmul(PS[pa][r0:r0 + 64, (h % 4) * 64:(h % 4 + 1) * 64], lhsT=kt[:, h, co * 64:(co + 1) * 64], rhs=qt[:, h, co * 64:(co + 1) * 64], start=True, stop=True),
                                     reads=[t_kt[h], t_qt[h]], writes=[PST[pa]])
                            S.op("dve", lambda r0=r0, half=half, pa=pa: nc.vector.tensor_tensor(out=Abf[r0:r0 + 64, half * 256:(half + 1) * 256].rearrange("p (h t) -> p h t", t=64), in0=PS[pa][r0:r0 + 64, 0:256].rearrange("p (h t) -> p h t", t=64),
                                                                                              in1=tri64[r0:r0 + 64, :].unsqueeze(1).to_broadcast([64, 4, 64]), op=ALU.mult), reads=[PST[pa], t_tri64], writes=[t_Abfh[half]])
                    for half in range(2):
                        hs = range(half * 4, half * 4 + 4)
                        pst = P_ST0 if half == 0 else P_ST1
                        for h in hs:
                            S.op("pe", lambda h=h, m=m, r0=r0, pst=pst: nc.tensor.matmul(PS[pst][:, (h % 4) * 128:(h % 4 + 1) * 128], lhsT=khT[r0:r0 + 64, h, m, :], rhs=vtok[r0:r0 + 64, m, h * 128:(h + 1) * 128], start=True, stop=True),
                                 reads=[t_khT[h], t_vtok[m]], writes=[PST[pst]])
                    for half in range(2):
                        hs = range(half * 4, half * 4 + 4)
                        pst = P_ST0 if half == 0 else P_ST1
                        if ownc:
                            for h in hs:
                                S.op("pe", lambda h=h, m=m, r0=r0: nc.tensor.matmul(PS[P_O][:, h * 64:(h + 1) * 64], lhsT=vtok[r0:r0 + 64, m, h * 128:(h + 1) * 128], rhs=Abf[r0:r0 + 64, h * 64:(h + 1) * 64], start=True, stop=False),
                                     reads=[t_vtok[m], t_Abfh[half]], writes=[t_PO[half]])
                                S.op("pe", lambda h=h, co=co: nc.tensor.matmul(PS[P_O][:, h * 64:(h + 1) * 64], lhsT=Sb[:, h, :], rhs=qt[:, h, co * 64:(co + 1) * 64], start=False, stop=True),
                                     reads=[t_Sbh[half], t_qt[h]], writes=[t_PO[half]])
                        for h in hs:
                            S.op("dve", lambda h=h, c=c, pst=pst: nc.vector.scalar_tensor_tensor(out=Sf[:, h, :], in0=Sf[:, h, :], scalar=ebl[:, h, c:c + 1], in1=PS[pst][:, (h % 4) * 128:(h % 4 + 1) * 128], op0=ALU.mult, op1=ALU.add),
                                 reads=[t_Sf[h], t_ebl[h], PST[pst]], writes=[t_Sf[h]])
                        S.op("act", lambda half=half: nc.scalar.copy(out=Sb[:, half * 4:(half + 1) * 4, :].rearrange("p h d -> p (h d)"), in_=Sf[:, half * 4:(half + 1) * 4, :].rearrange("p h d -> p (h d)")), reads=[t_Sf[h] for h in hs], writes=[t_Sbh[half]])
                    if ownc:
                        S.op("act", lambda co=co: nc.scalar.copy(out=obuf[:, co, :], in_=PS[P_O][:, :]), reads=t_PO, writes=[t_obuf[co]])
                    if g + 1 < NG_ALL:
                        nrm.piece(xg, t_xg, c)
                if g + 1 < NG_ALL:
                    nrm.rstd_()
                    nrm.apply(xg, t_xg, xn, t_xn)
                def onorm(co, g=g, sog=sog, t_sog=t_sog):
                    if co < 4:
                        S.op("act", lambda: nc.scalar.activation(out=osq[:], in_=obuf[:, co, :], func=AF.Square), reads=[t_obuf[co]], writes=[t_osq])
                        S.op("pe", lambda: nc.tensor.matmul(PS[2][:, :], lhsT=ones_bf[:], rhs=osq[:], start=True, stop=True), reads=[t_ones, t_osq], writes=[PST[2]])
                        S.op("act", lambda: nc.scalar.activation(out=olnv[:], in_=PS[2][:, :], func=AF.Ln, scale=1.0 / 128, bias=EPS), reads=[PST[2]], writes=[t_olnv])
                        S.op("act", lambda: nc.scalar.activation(out=orstd[:], in_=olnv[:], func=AF.Exp, scale=-0.5), reads=[t_olnv], writes=[t_orstd])
                        S.op("dve", lambda: nc.vector.tensor_tensor(out=otmp[:], in0=obuf[:, co, :], in1=orstd[:], op=ALU.mult), reads=[t_obuf[co], t_orstd], writes=[t_otmp])
                        S.op("dve", lambda: nc.vector.tensor_tensor(out=hbuf[:, :, co * 64:(co + 1) * 64], in0=otmp[:].rearrange("p (h t) -> p h t", t=64), in1=sog[:, :, co * 64:(co + 1) * 64], op=ALU.mult),
                             reads=[t_otmp] + t_sog, writes=[t_hbuf])
                    else:
                        S.dma("pool", Hsc.rearrange("(h p) t -> p h t", p=128)[:, :, g * 256:(g + 1) * 256], hbuf[:], reads=[t_hbuf], writes=[t_Hsc])
                pend["onorm"] = onorm
            for co in range(5):
                pend["onorm"](co)
            A.pop()

        ksum = A.alloc([128, 8, 32], F32); t_ksum = T()
        A.push()

        class Rope:
            def __init__(self):
                self.posi = A.alloc([128, G], I32); self.t_posi = T()
                self.ang = A.alloc([128, G], F32); self.t_ang = T()
                self.u = A.alloc([128, G], F32); self.t_u = T()
                self.ki = A.alloc([128, G], I32); self.t_ki = T()
                self.cosL = [A.alloc([128, G], F32) for _ in range(2)]; self.t_cosL = TL(2)
                self.sinL = [A.alloc([128, G], F32) for _ in range(2)]; self.t_sinL = TL(2)
                self.tsb = [A.alloc([128, G], F32) for _ in range(2)]; self.t_tsb = TL(2)
                self.a = [A.alloc([128, G], F32) for _ in range(2)]; self.t_a = TL(2)
                self.b = [A.alloc([128, G], F32) for _ in range(2)]; self.t_b = TL(2)

            def tables(self, c0, ts=0, c1=None):
                s = self
                s.cos, s.t_cos, s.sin, s.t_sin = s.cosL[ts], s.t_cosL[ts], s.sinL[ts], s.t_sinL[ts]
                if c1 is None:
                    S.dma("sp", s.posi[:], pos[0:1, c0:c0 + G].partition_broadcast(128), writes=[s.t_posi])
                else:
                    S.dma("sp", s.posi[:, 0:256], pos[0:1, c0:c0 + 256].partition_broadcast(128), writes=[s.t_posi])
                    S.dma("sp", s.posi[:, 256:512], pos[0:1, c1:c1 + 256].partition_broadcast(128), writes=[s.t_posi])
                S.op("dve", lambda: nc.vector.tensor_copy(out=s.ang[:], in_=s.posi[:]), reads=[s.t_posi], writes=[s.t_ang])
                S.op("dve", lambda: nc.vector.tensor_scalar(out=s.ang[:], in0=s.ang[:], scalar1=invf_sb[:, 0:1], scalar2=None, op0=ALU.mult), reads=[s.t_ang, t_invf], writes=[s.t_ang])
                for which in ("sin", "cos"):
                    dst, t_dst = (s.sin, s.t_sin) if which == "sin" else (s.cos, s.t_cos)
                    add = 0.0 if which == "sin" else 0.25
                    S.op("dve", lambda add=add: nc.vector.tensor_scalar(out=s.u[:], in0=s.ang[:], scalar1=float(1 / (2 * np.pi)), scalar2=add, op0=ALU.mult, op1=ALU.add), reads=[s.t_ang], writes=[s.t_u])
                    S.op("dve", lambda: nc.vector.tensor_copy(out=s.ki[:], in_=s.u[:]), reads=[s.t_u], writes=[s.t_ki])
                    S.op("dve", lambda: nc.vector.tensor_copy(out=s.u[:], in_=s.ki[:]), reads=[s.t_ki], writes=[s.t_u])
                    S.op("dve", lambda dst=dst: nc.vector.scalar_tensor_tensor(out=dst[:], in0=s.u[:], scalar=-C1, in1=s.ang[:], op0=ALU.mult, op1=ALU.add), reads=[s.t_u, s.t_ang], writes=[t_dst])
                    S.op("dve", lambda dst=dst: nc.vector.scalar_tensor_tensor(out=dst[:], in0=s.u[:], scalar=-C2, in1=dst[:], op0=ALU.mult, op1=ALU.add), reads=[s.t_u, t_dst], writes=[t_dst])
                    if which == "cos":
                        S.op("dve", lambda dst=dst: nc.vector.tensor_scalar(out=dst[:], in0=dst[:], scalar1=float(np.pi / 2), scalar2=None, op0=ALU.add), reads=[t_dst], writes=[t_dst])
                    S.op("dve", lambda dst=dst: nc.vector.tensor_scalar(out=dst[:], in0=dst[:], scalar1=PI_LO, scalar2=-PI_LO, op0=ALU.min, op1=ALU.max), reads=[t_dst], writes=[t_dst])
                    if which == "sin":
                        S.op("act", lambda dst=dst: nc.scalar.activation(out=dst[:], in_=dst[:], func=AF.Sin, scale=sgn_sb[:, 0:1]), reads=[t_dst, t_sgn], writes=[t_dst])
                    else:
                        S.op("act", lambda dst=dst: nc.scalar.activation(out=dst[:], in_=dst[:], func=AF.Sin), reads=[t_dst], writes=[t_dst])

            def part1(self, p, sl, ts=0):
                s = self
                cos_, t_cos_ = s.cosL[ts], s.t_cosL[ts]
                S.op("act", lambda: nc.scalar.copy(out=s.tsb[sl][:], in_=PS[p][:, :]), reads=[PST[p]], writes=[s.t_tsb[sl]])
                S.op("dve", lambda: nc.vector.tensor_tensor(out=s.a[sl][:], in0=s.tsb[sl][:], in1=cos_[:], op=ALU.mult), reads=[s.t_tsb[sl], t_cos_], writes=[s.t_a[sl]])

            def part2(self, sl, out_f32, t_out, ts=0):
                s = self
                sin_, t_sin_ = s.sinL[ts], s.t_sinL[ts]
                S.op("pe", lambda: nc.tensor.matmul(PS[5][:, :], lhsT=rotm_sb[:], rhs=s.tsb[sl][:], start=True, stop=True), reads=[t_rotm, s.t_tsb[sl]], writes=[PST[5]])
                S.op("dve", lambda: nc.vector.tensor_tensor(out=s.b[sl][:], in0=PS[5][:, :], in1=sin_[:], op=ALU.mult), reads=[PST[5], t_sin_], writes=[s.t_b[sl]])
                S.op("dve", lambda: nc.vector.tensor_tensor(out=out_f32, in0=s.a[sl][:], in1=s.b[sl][:], op=ALU.add), reads=[s.t_a[sl], s.t_b[sl]], writes=[t_out])

        def phase2():
            A.push()
            gm = gains["mix"]
            Wk = A.alloc([128, KC, D], BF16); t_Wk = WT(KC, D)
            Wv = A.alloc([128, KC, D], BF16); t_Wv = WT(KC, D)
            Wga = A.alloc([128, KC, 2 * D], BF16); t_Wga = WT(KC, 2 * D)
            load_weight(Wk, t_Wk, w_in, D, D, gm, col0=1024)
            load_weight(Wv, t_Wv, w_in, D, D, gm, col0=2048)
            load_weight(Wga, t_Wga, w_in, D, 2 * D, gm, col0=7168)
            xgL = [A.alloc([128, KC, G], F32) for _ in range(2)]; t_xgL = [TL(KC) for _ in range(2)]
            xnL = [A.alloc([128, KC, G], BF16) for _ in range(2)]; t_xnL = [TL(KC) for _ in range(2)]
            nrmL = [Norm(G, 2) for _ in range(2)]
            rp = Rope()
            kf = [A.alloc([128, G], F32) for _ in range(2)]; t_kf = TL(2)
            kbuf = A.alloc([128, 8, G], BF16); t_kbuf = T()
            vbuf = A.alloc([128, 2, D], BF16); t_vbuf = T()
            gbuf = A.alloc([128, 16, 256], BF16); t_gbuf = T()
            Kv = Ksc.rearrange("(j p) t -> p j t", p=128)
            Gv2 = Gsc.rearrange("(j p) t -> p j t", p=128)
            pp = {"i": 0}
            PB = (0, 1, 3, 4)

            def nextp():
                pp["i"] += 1
                return PB[pp["i"] % 4]

            def pro_early(g):
                gs = g % 2
                for kc in range(KC):
                    S.dma("sp", xgL[gs][:, kc, :], xT_v[:, kc, g * G:(g + 1) * G], writes=[t_xgL[gs][kc]])
                nrmL[gs].run(xgL[gs], t_xgL[gs], None, None, rstd_only=True)

            def pro_late(g):
                gs = g % 2
                nrmL[gs].apply(xgL[gs], t_xgL[gs], xnL[gs], t_xnL[gs])
                rp.tables(g * G, gs)

            n = 0
            pro_early(0)
            pro_late(0)
            for g in range(NG_ALL):
                own = True
                gs = g % 2
                xn, t_xn = xnL[gs], t_xnL[gs]
                if g + 1 < NG_ALL:
                    pro_early(g + 1)
                def k_finish(j, sl, g=g, gs=gs):
                    rp.part2(sl, kf[sl][:], t_kf[sl], gs)
                    for bk in range(2):
                        S.op("act", lambda bk=bk: nc.scalar.activation(out=kbuf[:, j, bk * 256:(bk + 1) * 256], in_=kf[sl][:, bk * 256:(bk + 1) * 256], func=AF.Copy, accum_out=ksum[:, j, 2 * g + bk:2 * g + bk + 1]),
                             reads=[t_kf[sl]], writes=[t_kbuf, t_ksum])
                    if j == 7:
                        S.dma("pool", Kv[:, :, g * G:(g + 1) * G], kbuf[:], reads=[t_kbuf], writes=[t_Ksc])

                prev = None
                for j in range(8):
                    p = nextp()
                    mm_fm(p, Wk, t_Wk, j * 128, xn, t_xn, G)
                    rp.part1(p, j % 2, gs)
                    if prev is not None:
                        k_finish(*prev)
                    prev = (j, j % 2)
                k_finish(*prev)
                if g + 1 < NG_ALL:
                    pro_late(g + 1)
                for tt in range(4):
                    for ch in range(2):
                        p = nextp()
                        mm_tm(p, xn, t_xn, tt * 128, Wv, t_Wv, ch * 512, 512)
                        S.op("act", lambda p=p, tt=tt, ch=ch: nc.scalar.copy(out=vbuf[:, tt % 2, ch * 512:(ch + 1) * 512], in_=PS[p][:, :]), reads=[PST[p]], writes=[t_vbuf])
                    if tt % 2 == 1:
                        r0_ = g * G + (tt - 1) * 128
                        S.dma("pool", Vsc[r0_:r0_ + 256, :].rearrange("(t p) f -> p t f", p=128), vbuf[:], reads=[t_vbuf], writes=[t_Vsc])
                for j in range(16):
                    p = nextp()
                    mm_fm(p, Wga, t_Wga, j * 128, xn, t_xn, 256, tok0=256)
                    S.op("act", lambda p=p, j=j: nc.scalar.activation(out=gbuf[:, j, :], in_=PS[p][:, 0:256], func=AF.Sigmoid), reads=[PST[p]], writes=[t_gbuf])
                    if j % 8 == 7:
                        S.dma("pool", Gv2[:, j - 7:j + 1, g * 256:(g + 1) * 256], gbuf[:, j - 7:j + 1, :], reads=[t_gbuf], writes=[t_Gsc])
            A.pop()

        def phase3a():
            A.push()
            gm = gains["mix"]
            Wq = A.alloc([128, KC, D], BF16); t_Wq = WT(KC, D)
            load_weight(Wq, t_Wq, w_in, D, D, gm, col0=0)
            xgL = [A.alloc([128, KC, G], F32) for _ in range(2)]; t_xgL = [TL(KC) for _ in range(2)]
            xnL = [A.alloc([128, KC, G], BF16) for _ in range(2)]; t_xnL = [TL(KC) for _ in range(2)]
            nrmL = [Norm(G, 2) for _ in range(2)]
            rp = Rope()
            qf = A.alloc([128, 8, G], F32); t_qf = TL(8)
            qbuf = A.alloc([128, 8, G], BF16); t_qbuf = T()
            Qv = Qsc.rearrange("(j p) t -> p j t", p=128)
            gmx = A.alloc([128, 16, 32], F32); t_gmx = T()
            top8 = A.alloc([128, 16, 8], F32); t_top8 = TL(16)
            sel = A.alloc([128, 16, 32], F32); t_sel = T()
            mbL = [A.alloc([128, 16, 32], BF16) for _ in range(2)]; t_mbL = TL(2)
            mbT = A.alloc([32, 16, G], BF16); t_mbT = T()
            pp = {"i": 0}
            PB = (0, 1, 6, 7)

            def nextp():
                pp["i"] += 1
                return PB[pp["i"] % 4]

            P_G, P_T = 3, 4
            ksbd = A.alloc([128, 8, 64], F32); t_ksbd = T()
            S.op("pool", lambda: nc.gpsimd.memset(ksbd[:], 0.0), writes=[t_ksbd])
            S.op("dve", lambda: nc.vector.tensor_copy(out=ksbd[0:64, :, 0:32], in_=ksum[0:64, :, :]), reads=[t_ksum, t_ksbd], writes=[t_ksbd])
            S.op("dve", lambda: nc.vector.tensor_copy(out=ksbd[64:128, :, 32:64], in_=ksum[64:128, :, :]), reads=[t_ksum, t_ksbd], writes=[t_ksbd])

            def pro_early(go):
                gs = go % 2
                ca, cb = (4 * go + 1) * 256, (4 * go + 3) * 256
                for kc in range(KC):
                    S.dma("sp", xgL[gs][:, kc, 0:256], xT_v[:, kc, ca:ca + 256], writes=[t_xgL[gs][kc]])
                    S.dma("sp", xgL[gs][:, kc, 256:512], xT_v[:, kc, cb:cb + 256], writes=[t_xgL[gs][kc]])
                nrmL[gs].run(xgL[gs], t_xgL[gs], None, None, rstd_only=True)

            def pro_late(go):
                gs = go % 2
                ca, cb = (4 * go + 1) * 256, (4 * go + 3) * 256
                nrmL[gs].apply(xgL[gs], t_xgL[gs], xnL[gs], t_xnL[gs])
                rp.tables(ca, gs, c1=cb)

            n = 0
            pro_early(0)
            pro_late(0)
            for go in range(TO // G):
                gs = go % 2
                xn, t_xn = xnL[gs], t_xnL[gs]
                if go + 1 < TO // G:
                    pro_early(go + 1)
                def q_finish(j, sl, go=go, gs=gs):
                    rp.part2(sl, qf[:, j, :], t_qf[j], gs)
                    S.op("act", lambda: nc.scalar.activation(out=qbuf[:, j, :], in_=qf[:, j, :], func=AF.Copy, scale=0.125), reads=[t_qf[j]], writes=[t_qbuf])
                    if j == 7:
                        S.dma("pool", Qv[:, :, go * G:(go + 1) * G], qbuf[:], reads=[t_qbuf], writes=[t_Qsc])

                prev = None
                for j in range(8):
                    p = nextp()
                    mm_fm(p, Wq, t_Wq, j * 128, xn, t_xn, G)
                    rp.part1(p, j % 2, gs)
                    if prev is not None:
                        q_finish(*prev)
                    prev = (j, j % 2)
                q_finish(*prev)
                if go + 1 < TO // G:
                    pro_late(go + 1)
                def gate_tr(qt_, mbs):
                    c0 = qt_ * 128
                    for hq in range(4):
                        for hh in range(4):
                            h = hq * 4 + hh
                            S.op("pe", lambda h=h, hh=hh: nc.tensor.matmul(PS[P_T][0:32, hh * 128:(hh + 1) * 128], lhsT=mbL[mbs][:, h, :], rhs=ident_bf[:], start=True, stop=True), reads=[t_mbL[mbs], t_ident], writes=[PST[P_T]])
                        S.op("act", lambda hq=hq: nc.scalar.copy(out=mbT[:, hq * 4:(hq + 1) * 4, c0:c0 + 128], in_=PS[P_T][0:32, :].rearrange("p (h q) -> p h q", q=128)), reads=[PST[P_T]], writes=[t_mbT])

                for qt_ in range(4):
                    qblk = go * 2 + qt_ // 2
                    c0 = qt_ * 128
                    mbs = qt_ % 2
                    for j in range(8):
                        S.op("pe", lambda j=j, c0=c0: nc.tensor.matmul(PS[P_G][:, j * 64:(j + 1) * 64], lhsT=qf[:, j, c0:c0 + 128], rhs=ksbd[:, j, :], start=True, stop=True),
                             reads=[t_qf[j], t_ksbd], writes=[PST[P_G]])
                    pn = pastneg_sb[:, qblk * 32:(qblk + 1) * 32].unsqueeze(1).to_broadcast([128, 16, 32])
                    S.op("dve", lambda pn=pn: nc.vector.tensor_tensor(out=gmx[:], in0=PS[P_G][:, :].rearrange("p (h n) -> p h n", n=32), in1=pn, op=ALU.add), reads=[PST[P_G], t_pastneg], writes=[t_gmx])
                    for h in range(16):
                        S.op("dve", lambda h=h: nc.vector.max(out=top8[:, h, :], in_=gmx[:, h, :]), reads=[t_gmx], writes=[t_top8[h]])
                    S.op("dve", lambda: nc.vector.tensor_tensor(out=sel[:], in0=gmx[:], in1=top8[:, :, 2:3].to_broadcast([128, 16, 32]), op=ALU.is_ge), reads=[t_gmx] + t_top8, writes=[t_sel])
                    S.op("dve", lambda pn=pn: nc.vector.scalar_tensor_tensor(out=sel[:], in0=sel[:], scalar=NEGM, in1=pn, op0=ALU.mult, op1=ALU.add), reads=[t_sel, t_pastneg], writes=[t_sel])
                    S.op("dve", lambda mbs=mbs: nc.vector.tensor_scalar(out=mbL[mbs][:], in0=sel[:], scalar1=-NEGM, scalar2=-NEGM, op0=ALU.add, op1=ALU.max), reads=[t_sel], writes=[t_mbL[mbs]])
                    if qt_ >= 1:
                        gate_tr(qt_ - 1, (qt_ - 1) % 2)
                gate_tr(3, 1)
                S.dma("pool", Msc.rearrange("h n t -> n h t")[:, :, go * G:(go + 1) * G], mbT[:], reads=[t_mbT], writes=[t_Msc])
            A.pop()

        def phase3b():
            A.push()
            kaug = [A.alloc([96, TT], BF16) for _ in range(2)]; t_kaug = TL(2)
            qaug = [A.alloc([96, TO], BF16) for _ in range(2)]; t_qq = TL(2); t_qm = TL(2)
            vaug = [A.alloc([128, 64, 65], BF16) for _ in range(2)]; t_va = [TL(8) for _ in range(2)]
            ind = A.alloc([32, TT], F32); t_ind = T()
            pt = [A.alloc([128, 512], BF16) for _ in range(4)]; t_pt = TL(4)
            ou = A.alloc([65, 256], F32); t_ou = T()
            rden = A.alloc([65, 256], F32); t_rden = T()
            onesb = A.alloc([65, 64], BF16); t_onesb = T()
            rhi = A.alloc([65, 256], BF16); t_rhi = T()
            rlo = A.alloc([65, 256], BF16); t_rlo = T()
            abuf = [A.alloc([64, TO], BF16) for _ in range(2)]; t_abuf = TL(2)
            S.op("pool", lambda: nc.gpsimd.memset(onesb[:], 1.0), writes=[t_onesb])
            S.op("pool", lambda: nc.gpsimd.memset(ind[:], 1.0), writes=[t_ind])
            S.op("pool", lambda: nc.gpsimd.affine_select(out=ind[:], in_=ind[:], pattern=[[1, TT]], compare_op=ALU.is_ge, fill=0.0, base=0, channel_multiplier=-256), reads=[t_ind], writes=[t_ind])
            S.op("pool", lambda: nc.gpsimd.affine_select(out=ind[:], in_=ind[:], pattern=[[-1, TT]], compare_op=ALU.is_ge, fill=0.0, base=255, channel_multiplier=256), reads=[t_ind], writes=[t_ind])
            for i in range(2):
                S.op("dve", lambda i=i: nc.vector.tensor_copy(out=kaug[i][64:96, :], in_=ind[:]), reads=[t_ind], writes=[t_kaug[i]])
                S.op("pool", lambda i=i: nc.gpsimd.memset(vaug[i][:, :, 64:65], 1.0), writes=t_va[i])
            P_S = (0, 1, 2, 6)
            P_OO = (3, 4)
            P_B = 5
            LOOK = 3
            Vv = Vsc.rearrange("(kt p) (h d) -> p kt h d", p=128, d=64)
            gu = {"u": 0, "o": 0}
            for h in range(16):
                bi = h % 2
                S.dma("sp", kaug[bi][0:64, :], Ksc[h * 64:(h + 1) * 64, :], reads=[t_Ksc], writes=[t_kaug[bi]])
                S.dma("sp", qaug[bi][0:64, :], Qsc[h * 64:(h + 1) * 64, :], reads=[t_Qsc], writes=[t_qq[bi]])
                S.dma("sp", qaug[bi][64:96, :], Msc[h], reads=[t_Msc], writes=[t_qm[bi]])
                for k8 in range(8):
                    S.dma("sp", vaug[bi][:, k8 * 8:(k8 + 1) * 8, 0:64], Vv[:, k8 * 8:(k8 + 1) * 8, h, :], reads=[t_Vsc], writes=[t_va[bi][k8]])
                ka, qa, va = kaug[bi], qaug[bi], vaug[bi]
                tka, tqq, tqm, tva = t_kaug[bi], t_qq[bi], t_qm[bi], t_va[bi]
                units = []
                for qb in range(16):
                    for n in range(2 * qb + 1):
                        units.append(("past", qb, n))
                    units.append(("own", qb, 2 * qb + 1))
                meta = {}
                pending = []

                def emit_qk(i, ka=ka, qa=qa, tka=tka, tqq=tqq, tqm=tqm):
                    kind, qb, n = units[i]
                    u = gu["u"]
                    gu["u"] += 1
                    bank = P_S[u % 4]
                    pti = u % 4
                    meta[i] = (bank, pti)
                    q0 = qb * 256
                    k0 = n * 256
                    if kind == "past":
                        for kc in range(2):
                            S.op("pe", lambda kc=kc, bank=bank, k0=k0, q0=q0: nc.tensor.matmul(PS[bank][:, kc * 256:(kc + 1) * 256], lhsT=ka[0:96, k0 + kc * 128:k0 + (kc + 1) * 128], rhs=qa[0:96, q0:q0 + 256], start=True, stop=True),
                                 reads=[tka, tqq, tqm], writes=[PST[bank]])
                        S.op("act", lambda bank=bank, pti=pti: nc.scalar.activation(out=pt[pti][:], in_=PS[bank][:, :], func=AF.Exp), reads=[PST[bank]], writes=[t_pt[pti]])
                    else:
                        S.op("pe", lambda bank=bank, k0=k0, q0=q0: nc.tensor.matmul(PS[bank][:, 0:256], lhsT=ka[0:64, k0:k0 + 128], rhs=qa[0:64, q0:q0 + 256], start=True, stop=True), reads=[tka, tqq], writes=[PST[bank]])
                        S.op("pe", lambda bank=bank, k0=k0, q0=q0: nc.tensor.matmul(PS[bank][:, 384:512], lhsT=ka[0:64, k0 + 128:k0 + 256], rhs=qa[0:64, q0 + 128:q0 + 256], start=True, stop=True), reads=[tka, tqq], writes=[PST[bank]])
                        S.op("act", lambda bank=bank, pti=pti: nc.scalar.activation(out=pt[pti][:, 0:256], in_=PS[bank][:, 0:256], func=AF.Exp), reads=[PST[bank]], writes=[t_pt[pti]])
                        S.op("act", lambda bank=bank, pti=pti: nc.scalar.activation(out=pt[pti][:, 384:512], in_=PS[bank][:, 384:512], func=AF.Exp), reads=[PST[bank]], writes=[t_pt[pti]])
                        S.op("dve", lambda pti=pti: nc.vector.tensor_tensor(out=pt[pti][:, 0:128], in0=pt[pti][:, 0:128], in1=tri01[:], op=ALU.mult), reads=[t_pt[pti], t_tri], writes=[t_pt[pti]])
                        S.op("dve", lambda pti=pti: nc.vector.tensor_tensor(out=pt[pti][:, 384:512], in0=pt[pti][:, 384:512], in1=tri01[:], op=ALU.mult), reads=[t_pt[pti], t_tri], writes=[t_pt[pti]])

                def emit_pv(i, va=va, tva=tva, bi=bi):
                    kind, qb, n = units[i]
                    bank, pti = meta.pop(i)
                    if kind == "past" and n == 0:
                        gu["o"] += 1
                    po = P_OO[gu["o"] % 2]
                    q0 = qb * 256
                    if kind == "past":
                        for kc in range(2):
                            kt = n * 2 + kc
                            S.op("pe", lambda po=po, pti=pti, kt=kt, kc=kc, st=(n == 0 and kc == 0): nc.tensor.matmul(PS[po][0:65, 0:256], lhsT=va[:, kt, :], rhs=pt[pti][:, kc * 256:(kc + 1) * 256], start=st, stop=False),
                                 reads=[tva[kt // 8], t_pt[pti]], writes=[PST[po]])
                    else:
                        kt = n * 2
                        S.op("pe", lambda po=po, pti=pti, kt=kt: nc.tensor.matmul(PS[po][0:65, 0:256], lhsT=va[:, kt, :], rhs=pt[pti][:, 0:256], start=False, stop=False), reads=[tva[kt // 8], t_pt[pti]], writes=[PST[po]])
                        S.op("pe", lambda po=po, pti=pti, kt=kt: nc.tensor.matmul(PS[po][0:65, 128:256], lhsT=va[:, kt + 1, :], rhs=pt[pti][:, 384:512], start=False, stop=True), reads=[tva[(kt + 1) // 8], t_pt[pti]], writes=[PST[po]])
                        S.op("dve", lambda po=po: nc.vector.tensor_copy(out=ou[:], in_=PS[po][0:65, 0:256]), reads=[PST[po]], writes=[t_ou])
                        S.op("dve", lambda: nc.vector.reciprocal(out=rden[64:65, :], in_=ou[64:65, :]), reads=[t_ou], writes=[t_rden])
                        S.op("dve", lambda: nc.vector.tensor_copy(out=rhi[64:65, :], in_=rden[64:65, :]), reads=[t_rden], writes=[t_rhi])
                        S.op("dve", lambda: nc.vector.tensor_tensor(out=rlo[64:65, :], in0=rden[64:65, :], in1=rhi[64:65, :], op=ALU.subtract), reads=[t_rden, t_rhi], writes=[t_rlo])

                        def part_b(bi=bi, q0=q0):
                            S.op("pe", lambda: nc.tensor.matmul(PS[P_B][0:64, 0:256], lhsT=onesb[64:65, :], rhs=rhi[64:65, :], start=True, stop=False), reads=[t_onesb, t_rhi], writes=[PST[P_B]])
                            S.op("pe", lambda: nc.tensor.matmul(PS[P_B][0:64, 0:256], lhsT=onesb[64:65, :], rhs=rlo[64:65, :], start=False, stop=True), reads=[t_onesb, t_rlo], writes=[PST[P_B]])
                            S.op("dve", lambda bi=bi, q0=q0: nc.vector.tensor_tensor(out=abuf[bi][:, q0:q0 + 256], in0=ou[0:64, :], in1=PS[P_B][0:64, 0:256], op=ALU.mult), reads=[t_ou, PST[P_B]], writes=[t_abuf[bi]])
                        pending.append((i + 3, part_b))

                nu = len(units)
                for i in range(nu + LOOK):
                    if i < nu:
                        emit_qk(i)
                    j = i - LOOK
                    if j >= 0:
                        emit_pv(j)
                    while pending and pending[0][0] <= j:
                        pending.pop(0)[1]()
                while pending:
                    pending.pop(0)[1]()
                S.dma("pool", Asc[h * 64:(h + 1) * 64, :], abuf[bi][:], reads=[t_abuf[bi]], writes=[t_Asc])
            A.pop()

        def phase4a():
            A.push()
            Wa = A.alloc([128, KC, D], BF16); t_Wa = WT(KC, D)
            Wh = A.alloc([128, KC, D], BF16); t_Wh = WT(KC, D)
            Wo = A.alloc([128, KC, D], BF16); t_Wo = WT(KC, D)
            Wqx = A.alloc([128, KC, D], BF16); t_Wqx = WT(KC, D)
            Wox = A.alloc([128, KC, D], BF16); t_Wox = WT(KC, D)
            kxT = A.alloc([128, KC, NMEM], BF16); t_kxT = T()
            vx = A.alloc([128, 2, D], BF16); t_vx = T()
            load_weight(Wa, t_Wa, w_bra, D, D)
            load_weight(Wh, t_Wh, w_brh, D, D)
            load_weight(Wo, t_Wo, w_out, D, D)
            load_weight(Wqx, t_Wqx, wq_x, D, D, gains["x"])
            load_weight(Wox, t_Wox, wo_x, D, D)
            A.push()
            Wkv = A.alloc([128, KC, 2 * D], BF16); t_Wkv = WT(KC, 2 * D)
            load_weight(Wkv, t_Wkv, wkv_x, D, 2 * D, gains["mem"])
            mg_ = A.alloc([128, KC, NMEM], F32); t_mg = TL(KC)
            mn = A.alloc([128, KC, NMEM], BF16); t_mn = TL(KC)
            nm = Norm(NMEM, 2)
            memT_v = memT.rearrange("(kc p) t -> p kc t", p=128)
            for kc in range(KC):
                S.dma("sp", mg_[:, kc, :], memT_v[:, kc, :], writes=[t_mg[kc]])
            nm.run(mg_, t_mg, mn, t_mn)
            for oc in range(8):
                p = oc % 2
                mm_fm(p, Wkv, t_Wkv, oc * 128, mn, t_mn, NMEM)
                S.op("act", lambda p=p, oc=oc: nc.scalar.copy(out=kxT[:, oc, :], in_=PS[p][:, 0:NMEM]), reads=[PST[p]], writes=[t_kxT])
            for mt in range(2):
                for ch in range(2):
                    p = (mt * 2 + ch) % 2
                    mm_tm(p, mn, t_mn, mt * 128, Wkv, t_Wkv, D + ch * 512, 512)
                    S.op("act", lambda p=p, mt=mt, ch=ch: nc.scalar.copy(out=vx[:, mt, ch * 512:(ch + 1) * 512], in_=PS[p][:, :]), reads=[PST[p]], writes=[t_vx])
            A.pop()
            S.barrier()
            ag = A.alloc([128, KC, G], BF16); t_ag = TL(KC)
            hg = A.alloc([128, KC, G], BF16); t_hg = TL(KC)
            gg = [A.alloc([128, 2, G], BF16) for _ in range(2)]; t_gg = TL(2)
            xg = [A.alloc([128, G], F32) for _ in range(2)]; t_xg = TL(2)
            mgd = A.alloc([128, KC, G], BF16); t_mgd = TL(KC)
            tmp = A.alloc([128, G], F32); t_tmp = T()
            tmp2 = A.alloc([128, G], F32); t_tmp2 = T()
            h1 = A.alloc([128, KC, G], F32); t_h1 = TL(KC)
            hn = A.alloc([128, KC, G], BF16); t_hn = TL(KC)
            qx = A.alloc([128, KC, G], BF16); t_qx = TL(KC)
            ptx = [[A.alloc([128, G], BF16) for _ in range(2)] for _ in range(2)]; t_ptx = [TL(2) for _ in range(2)]
            rdx = [A.alloc([128, G], F32) for _ in range(2)]; t_rdx = TL(2)
            ox = A.alloc([128, KC, G], BF16); t_ox = TL(KC)
            nrm = Norm(G, 2)
            Av = Asc.rearrange("(kc p) t -> p kc t", p=128)
            Hv = Hsc.rearrange("(kc p) t -> p kc t", p=128)
            Gv = Gsc.rearrange("(s kc p) t -> p s kc t", p=128, s=2)
            H2v = H2sc.rearrange("(kc p) t -> p kc t", p=128)
            pp = {"i": 0}

            def nextp():
                pp["i"] += 1
                return pp["i"] % 2

            n2 = 0
            n3 = 0
            for go in range(TO // G):
                ca, cb = (4 * go + 1) * 256, (4 * go + 3) * 256
                cs = slice(go * G, (go + 1) * G)
                for kc in range(KC):
                    S.dma("sp", ag[:, kc, :], Av[:, kc, cs], reads=[t_Asc], writes=[t_ag[kc]])
                    S.dma("sp", hg[:, kc, :], Hv[:, kc, cs], reads=[t_Hsc], writes=[t_hg[kc]])
                for oc in range(KC):
                    gi = n3 % 2
                    n3 += 1
                    S.dma("sp", gg[gi][:], Gv[:, :, oc, cs], reads=[t_Gsc], writes=[t_gg[gi]])
                    p = nextp()
                    mm_fm(p, Wa, t_Wa, oc * 128, ag, t_ag, G)
                    S.op("dve", lambda p=p, gi=gi: nc.vector.tensor_tensor(out=tmp[:], in0=PS[p][:, :], in1=gg[gi][:, 0, :], op=ALU.mult), reads=[PST[p], t_gg[gi]], writes=[t_tmp])
                    p = nextp()
                    mm_fm(p, Wh, t_Wh, oc * 128, hg, t_hg, G)
                    S.op("dve", lambda p=p, gi=gi: nc.vector.tensor_tensor(out=tmp2[:], in0=PS[p][:, :], in1=gg[gi][:, 1, :], op=ALU.mult), reads=[PST[p], t_gg[gi]], writes=[t_tmp2])
                    S.op("dve", lambda oc=oc: nc.vector.tensor_tensor(out=mgd[:, oc, :], in0=tmp2[:], in1=tmp[:], op=ALU.add), reads=[t_tmp2, t_tmp], writes=[t_mgd[oc]])
                for oc in range(KC):
                    xi = n3 % 2
                    n3 += 1
                    S.dma("sp", xg[xi][:, 0:256], xT_v[:, oc, ca:ca + 256], writes=[t_xg[xi]])
                    S.dma("sp", xg[xi][:, 256:512], xT_v[:, oc, cb:cb + 256], writes=[t_xg[xi]])
                    p = nextp()
                    mm_fm(p, Wo, t_Wo, oc * 128, mgd, t_mgd, G)
                    S.op("dve", lambda p=p, oc=oc, xi=xi: nc.vector.tensor_tensor(out=h1[:, oc, :], in0=PS[p][:, :], in1=xg[xi][:], op=ALU.add), reads=[PST[p], t_xg[xi]], writes=[t_h1[oc]])
                nrm.run(h1, t_h1, hn, t_hn)
                for oc in range(KC):
                    p = nextp()
                    mm_fm(p, Wqx, t_Wqx, oc * 128, hn, t_hn, G)
                    S.op("act", lambda p=p, oc=oc: nc.scalar.copy(out=qx[:, oc, :], in_=PS[p][:, :]), reads=[PST[p]], writes=[t_qx[oc]])
                def xa_scores(hx):
                    for mt in range(2):
                        p = 3 + mt
                        for dc in range(2):
                            S.op("pe", lambda p=p, mt=mt, dc=dc: nc.tensor.matmul(PS[p][:, :], lhsT=kxT[:, hx * 2 + dc, mt * 128:(mt + 1) * 128], rhs=qx[:, hx * 2 + dc, :], start=(dc == 0), stop=(dc == 1)),
                                 reads=[t_kxT, t_qx[hx * 2 + dc]], writes=[PST[p]])
                        S.op("act", lambda p=p, mt=mt: nc.scalar.activation(out=ptx[hx % 2][mt][:], in_=PS[p][:, :], func=AF.Exp, scale=1.0 / 16), reads=[PST[p]], writes=[t_ptx[hx % 2][mt]])

                def xa_finish(hx):
                    sl_ = hx % 2
                    for mt in range(2):
                        S.op("pe", lambda mt=mt: nc.tensor.matmul(PS[5][:, :], lhsT=ones_bf[:], rhs=ptx[sl_][mt][:], start=(mt == 0), stop=(mt == 1)), reads=[t_ones, t_ptx[sl_][mt]], writes=[PST[5]])
                    S.op("act", lambda: nc.scalar.activation(out=rdx[sl_][:], in_=PS[5][:, :], func=AF.Ln), reads=[PST[5]], writes=[t_rdx[sl_]])
                    S.op("act", lambda: nc.scalar.activation(out=rdx[sl_][:], in_=rdx[sl_][:], func=AF.Exp, scale=-1.0), reads=[t_rdx[sl_]], writes=[t_rdx[sl_]])
                    for dc in range(2):
                        p = 6 + dc
                        for mt in range(2):
                            S.op("pe", lambda p=p, dc=dc, mt=mt: nc.tensor.matmul(PS[p][:, :], lhsT=vx[:, mt, (hx * 2 + dc) * 128:(hx * 2 + dc + 1) * 128], rhs=ptx[sl_][mt][:], start=(mt == 0), stop=(mt == 1)),
                                 reads=[t_vx, t_ptx[sl_][mt]], writes=[PST[p]])
                        S.op("dve", lambda p=p, dc=dc: nc.vector.tensor_tensor(out=ox[:, hx * 2 + dc, :], in0=PS[p][:, :], in1=rdx[sl_][:], op=ALU.mult), reads=[PST[p], t_rdx[sl_]], writes=[t_ox[hx * 2 + dc]])

                for hx in range(4):
                    xa_scores(hx)
                    if hx >= 1:
                        xa_finish(hx - 1)
                xa_finish(3)
                for oc in range(KC):
                    p = nextp()
                    mm_fm(p, Wox, t_Wox, oc * 128, ox, t_ox, G)
                    S.op("dve", lambda p=p, oc=oc: nc.vector.tensor_tensor(out=h1[:, oc, :], in0=PS[p][:, :], in1=h1[:, oc, :], op=ALU.add), reads=[PST[p], t_h1[oc]], writes=[t_h1[oc]])
                    if oc % 4 == 3:
                        S.dma("pool", H2v[:, oc - 3:oc + 1, cs], h1[:, oc - 3:oc + 1, :], reads=t_h1[oc - 3:oc + 1], writes=[t_H2])
            A.pop()

        def phase4b():
            A.push()
            GF = 256
            Wfi = A.alloc([128, KC, 2 * FH], BF16); t_Wfi = WT(KC, 2 * FH)
            Wfo = A.alloc([128, FC, D], BF16); t_Wfo = WT(FC, D)
            load_weight(Wfi, t_Wfi, w_fi, D, 2 * FH, gains["ffn"])
            load_weight(Wfo, t_Wfo, w_fo, FH, D)
            hg_ = A.alloc([128, KC, GF], F32); t_hg = TL(KC)
            hn = A.alloc([128, KC, GF], BF16); t_hn = TL(KC)
            act = A.alloc([128, FC, GF], BF16); t_act = TL(FC)
            sl = A.alloc([128, GF], F32); t_sl = T()
            h3 = A.alloc([128, KC, GF], F32); t_h3 = TL(KC)
            nrm = Norm(GF, 2)
            nrm2 = Norm(GF, 3)
            gfin, t_gfin = gains["fin"]
            H2v = H2sc.rearrange("(kc p) t -> p kc t", p=128)
            Ov = outT.rearrange("(kc p) t -> p kc t", p=128)
            pp = {"i": 0}

            def nextp():
                pp["i"] += 1
                return pp["i"] % 2

            n2 = 0
            for go in range(TO // GF):
                cs = slice(go * GF, (go + 1) * GF)
                for kc in range(KC):
                    S.dma("sp", hg_[:, kc, :], H2v[:, kc, cs], reads=[t_H2], writes=[t_hg[kc]])
                nrm.run(hg_, t_hg, hn, t_hn)
                for j in range(FC):
                    pg = 4 + (j % 2)
                    pu = 6 + (j % 2)
                    mm_fm(pg, Wfi, t_Wfi, (2 * j) * 128, hn, t_hn, GF)
                    mm_fm(pu, Wfi, t_Wfi, (2 * j + 1) * 128, hn, t_hn, GF)
                    S.op("act", lambda pg=pg: nc.scalar.activation(out=sl[:], in_=PS[pg][:, 0:GF], func=AF.Silu), reads=[PST[pg]], writes=[t_sl])
                    S.op("dve", lambda pu=pu, j=j: nc.vector.tensor_tensor(out=act[:, j, :], in0=PS[pu][:, 0:GF], in1=sl[:], op=ALU.mult), reads=[PST[pu], t_sl], writes=[t_act[j]])
                for oc in range(KC):
                    p = nextp()
                    mm_fm(p, Wfo, t_Wfo, oc * 128, act, t_act, GF, kcn=FC)
                    S.op("dve", lambda p=p, oc=oc: nc.vector.tensor_tensor(out=h3[:, oc, :], in0=PS[p][:, 0:GF], in1=hg_[:, oc, :], op=ALU.add), reads=[PST[p], t_hg[oc]], writes=[t_h3[oc]])
                nrm2.run(h3, t_h3, None, None, rstd_only=True)
                for oc in range(KC):
                    S.op("dve", lambda oc=oc: nc.vector.scalar_tensor_tensor(out=h3[:, oc, :], in0=h3[:, oc, :], scalar=gfin[:, oc:oc + 1], in1=nrm2.rstd[:], op0=ALU.mult, op1=ALU.mult),
                         reads=[t_h3[oc], t_gfin, nrm2.t_rstd], writes=[t_h3[oc]])
                S.dma("pool", Ov[:, :, cs], h3[:], reads=t_h3, writes=[t_out])
            A.pop()

        phases = [phase1, phase2, phase3a, phase3b, phase4a, phase4b]
        for i, ph in enumerate(phases):
            if i < stop_after:
                S.barrier()
                ph()
        S.finish([t_Ksc, t_Vsc, t_Qsc, t_Msc, t_Gsc, t_Hsc, t_Asc, t_H2, t_out])
        build_program.n_inst = S.n_inst
    return nc


_NC_CACHE = {}


def make_in_maps(x, mem, positions, norm_mix_g, w_in, hgrn_lb_logits, hgrn_norm_g, w_br_attn,
                 w_br_hgrn, w_out, norm_x_g, norm_mem_g, wq_x, wkv_x, wo_x, norm_ffn_g,
                 w_ffn_in, w_ffn_out, final_norm_g):
    f32 = np.float32

    def gl(g, w=KC):
        return np.ascontiguousarray(np.asarray(g, f32).reshape(w, 128).T)

    inv_freq = np.power(np.float32(10000.0), -np.arange(0, 64, 2, dtype=np.float32) / np.float32(64)).astype(f32)
    invf = np.tile(inv_freq, 4).reshape(128, 1).astype(f32)
    sgn = np.tile(np.concatenate([-np.ones(32, f32), np.ones(32, f32)]), 2).reshape(128, 1)
    rotm = np.zeros((128, 128), f32)
    rotm[np.arange(128), np.arange(128) ^ 32] = 1.0
    lbl = np.asarray(hgrn_lb_logits, f32)
    lbl_l = np.concatenate([gl(lbl[0], 8), gl(lbl[1], 8)], axis=1)
    shared = {
        "invf": invf, "sgn": sgn, "rotm": rotm,
        "w_in": np.ascontiguousarray(w_in[0], dtype=f32), "g_mix": gl(norm_mix_g[0]),
        "lbl": np.ascontiguousarray(lbl_l), "g_hn": gl(hgrn_norm_g[0], 8),
        "w_bra": np.ascontiguousarray(w_br_attn[0], dtype=f32), "w_brh": np.ascontiguousarray(w_br_hgrn[0], dtype=f32),
        "w_out": np.ascontiguousarray(w_out[0], dtype=f32), "g_x": gl(norm_x_g[0]), "g_mem": gl(norm_mem_g[0]),
        "wq_x": np.ascontiguousarray(wq_x[0], dtype=f32), "wkv_x": np.ascontiguousarray(wkv_x[0], dtype=f32),
        "wo_x": np.ascontiguousarray(wo_x[0], dtype=f32), "g_ffn": gl(norm_ffn_g[0]),
        "w_fi": np.ascontiguousarray(np.asarray(w_ffn_in[0], f32).reshape(D, 2, FC, 128).transpose(0, 2, 1, 3).reshape(D, 2 * FH)), "w_fo": np.ascontiguousarray(w_ffn_out[0], dtype=f32),
        "g_fin": gl(final_norm_g),
    }
    in_maps = []
    for c in range(8):
        b, half = c // 2, c % 2
        xs = np.zeros((TT, D), f32)
        ps_ = np.zeros((TT,), np.int32)
        if half == 1:
            xs[:] = np.asarray(x[b], f32)
            ps_[:] = np.asarray(positions[b], np.int32)
        else:
            xs[256:] = np.asarray(x[b, 0:TT - 256], f32)
            ps_[256:] = np.asarray(positions[b, 0:TT - 256], np.int32)
        xT = np.ascontiguousarray(xs.T)
        pos = ps_.reshape(1, TT)
        pastneg = np.full((16, 32), -1e30, f32)
        for qb in range(16):
            lo = 0 if half == 1 else 1
            pastneg[qb, lo:2 * qb + 1] = 0.0
        m = dict(shared)
        m.update({"xT": xT, "pos": pos, "pastneg": pastneg.reshape(1, 512),
                  "memT": np.ascontiguousarray(np.asarray(mem[b], f32).T)})
        in_maps.append(m)
    return in_maps


def kernel(**inputs):
    if "nc" not in _NC_CACHE:
        _NC_CACHE["nc"] = build_program()
    nc = _NC_CACHE["nc"]
    in_maps = make_in_maps(**inputs)
    res = run_bass_kernel_spmd(nc, in_maps, core_ids=list(range(8)))
    out = np.empty((4, 2 * TO, D), np.float32)
    for c in range(8):
        b, half = c // 2, c % 2
        o = res.results[c]["outT"].T.reshape(16, 256, D)
        out[b].reshape(16, 2, 256, D)[:, half] = o
    return out
```
